# Optimizing a Trainium2 kernel written in Bass

```python
import jax, jax.numpy as jnp
from jax import lax
import numpy as np

D_MODEL = 1024
BATCH = 8
SEQ = 4096
DEPTH = 1
DEC_BATCH = 128
DEC_SEQ = 8
PAST_LEN = 16384
PAGE_SIZE = 128

HEAD_DIM = 64
D_RWKV = D_MODEL // 2
D_ATTN = D_MODEL - D_RWKV
N_RWKV_HEADS = D_RWKV // HEAD_DIM
N_Q_HEADS = D_ATTN // HEAD_DIM
N_KV_HEADS = 2
Q_PER_KV = N_Q_HEADS // N_KV_HEADS
WINDOW = 128
D_DECAY_LORA = 64
D_AAA_LORA = 64
D_PLE = 256
D_SHIFT = 3 * D_RWKV + D_DECAY_LORA + D_AAA_LORA
D_KV = N_KV_HEADS * HEAD_DIM
D_IN = D_SHIFT + D_RWKV + D_ATTN + 2 * D_KV + D_ATTN
NORM_EPS = 1e-6
LNX_EPS = 64e-5
NEG_INF = -1e30

kernel_name = 'hymba_rwkv7_swa_sink_step'


def _f32(t):
    return t.astype(jnp.float32)


def _rms(x, g, eps=NORM_EPS):
    xf = _f32(x)
    y = xf * lax.rsqrt(jnp.mean(xf * xf, axis=-1, keepdims=True) + eps)
    return (y * _f32(g)).astype(x.dtype)


def _in_proj(x, g_norm, w_in):
    h = _rms(x, g_norm) @ w_in
    sizes = [D_SHIFT, D_RWKV, D_ATTN, D_KV, D_KV, D_ATTN]
    idx = np.cumsum(sizes)[:-1].tolist()
    return jnp.split(h, idx, axis=-1)


def _rwkv_branch(f, prev, s0, mu, w0, w_dec2, a0, w_a2, k_k, k_a, r_k, lnx_w, lnx_b):
    B, T, _ = f.shape
    f_prev = jnp.concatenate([prev.astype(f.dtype), f[:, :-1]], axis=1)
    fs = _f32(f + (f_prev - f) * mu)
    r, k, v, wl, al = jnp.split(fs, [D_RWKV, 2 * D_RWKV, 3 * D_RWKV, 3 * D_RWKV + D_DECAY_LORA], axis=-1)
    logw = -jax.nn.softplus(-(_f32(w0) + jnp.tanh(wl) @ _f32(w_dec2))) - 0.5
    decay = jnp.exp(-jnp.exp(logw))
    a = jax.nn.sigmoid(_f32(a0) + al @ _f32(w_a2))
    hs = lambda t: t.reshape(B, T, N_RWKV_HEADS, HEAD_DIM)
    kk = hs(k * _f32(k_k))
    kk = kk * lax.rsqrt(jnp.maximum(jnp.sum(kk * kk, axis=-1, keepdims=True), 1e-24))
    k = k * (1.0 + (a - 1.0) * _f32(k_a))
    r_h, k_h, v_h, w_h, a_h = hs(r), hs(k), hs(v), hs(decay), hs(a)
    aa = -kk
    bb = kk * a_h

    def step(S, xs):
        r_t, w_t, k_t, v_t, a_t, b_t = xs
        Sa = jnp.einsum('bhvk,bhk->bhv', S, a_t)
        S = S * w_t[:, :, None, :] + Sa[..., None] * b_t[:, :, None, :] + v_t[..., None] * k_t[:, :, None, :]
        return S, jnp.einsum('bhvk,bhk->bhv', S, r_t)

    tm = lambda t: jnp.swapaxes(t, 0, 1)
    S_T, y = lax.scan(step, _f32(s0), (tm(r_h), tm(w_h), tm(k_h), tm(v_h), tm(aa), tm(bb)))
    y = tm(y)
    mean = jnp.mean(y, axis=-1, keepdims=True)
    var = jnp.mean(jnp.square(y - mean), axis=-1, keepdims=True)
    y = ((y - mean) * lax.rsqrt(var + LNX_EPS)).reshape(B, T, D_RWKV) * _f32(lnx_w) + _f32(lnx_b)
    bonus = jnp.sum(r_h * k_h * _f32(r_k), axis=-1, keepdims=True) * v_h
    y = y + bonus.reshape(B, T, D_RWKV)
    return y, S_T, f[:, -1:]


def _qkv(q, k, v, q_norm_w, k_norm_w):
    B, T, _ = q.shape
    q = _rms(q.reshape(B, T, N_KV_HEADS, Q_PER_KV, HEAD_DIM), q_norm_w)
    k = _rms(k.reshape(B, T, N_KV_HEADS, HEAD_DIM), k_norm_w)
    v = v.reshape(B, T, N_KV_HEADS, HEAD_DIM)
    return q, k, v


def _sink_attention(q, k, v, valid, sinks):
    s = jnp.einsum('...qkgd,...jkd->...kgqj', _f32(q), _f32(k)) * (HEAD_DIM ** -0.5)
    s = jnp.where(valid, s, NEG_INF)
    sink = _f32(sinks).reshape(N_KV_HEADS, Q_PER_KV, 1, 1)
    m = jnp.maximum(jnp.max(s, axis=-1, keepdims=True), sink)
    p = jnp.exp(s - m)
    denom = jnp.sum(p, axis=-1, keepdims=True) + jnp.exp(sink - m)
    return jnp.einsum('...kgqj,...jkd->...qkgd', (p / denom).astype(v.dtype), v)


def _swa_prompt(q, k, v, sinks):
    B, S = q.shape[:2]
    nb = S // WINDOW
    qb = q.reshape(B, nb, WINDOW, N_KV_HEADS, Q_PER_KV, HEAD_DIM)
    kb = k.reshape(B, nb, WINDOW, N_KV_HEADS, HEAD_DIM)
    vb = v.reshape(B, nb, WINDOW, N_KV_HEADS, HEAD_DIM)
    pad = jnp.zeros_like(kb[:, :1])
    kband = jnp.concatenate([jnp.concatenate([pad, kb[:, :-1]], axis=1), kb], axis=2)
    vband = jnp.concatenate([jnp.concatenate([pad, vb[:, :-1]], axis=1), vb], axis=2)
    blk = jnp.arange(nb)[:, None] * WINDOW
    qpos = blk + jnp.arange(WINDOW)[None]
    kpos = blk + jnp.arange(2 * WINDOW)[None] - WINDOW
    dist = qpos[:, :, None] - kpos[:, None, :]
    valid = (dist >= 0) & (dist < WINDOW) & (kpos[:, None, :] >= 0)
    o = _sink_attention(qb, kband, vband, valid[:, None, None], sinks)
    return o.reshape(B, S, D_ATTN)


def _swa_sample(q, k, v, ck, cv, sinks):
    B, T = q.shape[:2]
    wb = ck.shape[1]
    keys = jnp.concatenate([ck.astype(k.dtype), k], axis=1)
    vals = jnp.concatenate([cv.astype(v.dtype), v], axis=1)
    qpos = PAST_LEN + jnp.arange(T)
    kpos = PAST_LEN - wb + jnp.arange(wb + T)
    dist = qpos[:, None] - kpos[None, :]
    valid = (dist >= 0) & (dist < WINDOW)
    o = _sink_attention(q, keys, vals, valid, sinks)
    return o.reshape(B, T, D_ATTN), keys[:, -wb:], vals[:, -wb:]


def _merge(x, o_r, z_r, o_a, z_a, w_out, pl, g_ple, w_ple_gate, w_ple_proj):
    o = jnp.concatenate([o_r.astype(x.dtype) * jax.nn.silu(z_r), o_a.astype(x.dtype) * jax.nn.silu(z_a)], axis=-1)
    h = x + o @ w_out
    gate = jax.nn.sigmoid(_rms(h, g_ple) @ w_ple_gate)
    return h + gate * (pl.astype(x.dtype) @ w_ple_proj)


def setup_inputs(seed: int = 0) -> dict:
    key = jax.random.key(seed)
    ks = iter(jax.random.split(key, 40))
    nrm = lambda shape, s: jax.random.normal(next(ks), shape, jnp.float32) * s
    uni = lambda shape, lo, hi: jax.random.uniform(next(ks), shape, jnp.float32, lo, hi)
    wb = min(WINDOW, PAST_LEN)
    return {
        'x_prompt': nrm((BATCH, SEQ, D_MODEL), 1.0),
        'x_sample': nrm((DEC_BATCH, DEC_SEQ, D_MODEL), 1.0),
        'state_rwkv': nrm((DEPTH, DEC_BATCH, N_RWKV_HEADS, HEAD_DIM, HEAD_DIM), 1.0),
        'state_shift': nrm((DEPTH, DEC_BATCH, 1, D_SHIFT), 1.0),
        'cache_k': nrm((DEPTH, DEC_BATCH, wb, N_KV_HEADS, HEAD_DIM), 1.0),
        'cache_v': nrm((DEPTH, DEC_BATCH, wb, N_KV_HEADS, HEAD_DIM), 1.0),
        'p_prompt': nrm((DEPTH, BATCH, SEQ, D_PLE), 1.0),
        'p_sample': nrm((DEPTH, DEC_BATCH, DEC_SEQ, D_PLE), 1.0),
        'g_norm': 1.0 + nrm((DEPTH, D_MODEL), 0.02),
        'w_in': nrm((DEPTH, D_MODEL, D_IN), D_MODEL ** -0.5),
        'mu_shift': uni((DEPTH, D_SHIFT), 0.0, 1.0),
        'w0': uni((DEPTH, D_RWKV), -6.0, -1.0),
        'w_dec2': nrm((DEPTH, D_DECAY_LORA, D_RWKV), 0.1),
        'a0': nrm((DEPTH, D_RWKV), 0.5),
        'w_a2': nrm((DEPTH, D_AAA_LORA, D_RWKV), 0.1),
        'k_k': 0.85 + nrm((DEPTH, D_RWKV), 0.05),
        'k_a': 1.0 + nrm((DEPTH, D_RWKV), 0.05),
        'r_k': nrm((DEPTH, N_RWKV_HEADS, HEAD_DIM), 0.1),
        'lnx_w': 1.0 + nrm((DEPTH, D_RWKV), 0.02),
        'lnx_b': nrm((DEPTH, D_RWKV), 0.02),
        'q_norm_w': 1.0 + nrm((DEPTH, HEAD_DIM), 0.02),
        'k_norm_w': 1.0 + nrm((DEPTH, HEAD_DIM), 0.02),
        'sinks': nrm((DEPTH, N_Q_HEADS), 0.5),
        'w_out': nrm((DEPTH, D_MODEL, D_MODEL), D_MODEL ** -0.5),
        'g_ple': 1.0 + nrm((DEPTH, D_MODEL), 0.02),
        'w_ple_gate': nrm((DEPTH, D_MODEL, D_MODEL), D_MODEL ** -0.5),
        'w_ple_proj': nrm((DEPTH, D_PLE, D_MODEL), D_PLE ** -0.5),
    }


def reference(x_prompt, x_sample, state_rwkv, state_shift, cache_k, cache_v, p_prompt, p_sample,
              g_norm, w_in, mu_shift, w0, w_dec2, a0, w_a2, k_k, k_a, r_k, lnx_w, lnx_b,
              q_norm_w, k_norm_w, sinks, w_out, g_ple, w_ple_gate, w_ple_proj):
    xp, xs = x_prompt, x_sample
    bp, bs = xp.shape[0], xs.shape[0]
    s_p_list, s_s_list, sh_p_list, sh_s_list = [], [], [], []
    kp_list, ks_list, vp_list, vs_list = [], [], [], []
    for i in range(DEPTH):
        rw = (mu_shift[i], w0[i], w_dec2[i], a0[i], w_a2[i], k_k[i], k_a[i], r_k[i], lnx_w[i], lnx_b[i])
        f, z_r, q, k, v, z_a = _in_proj(xp, g_norm[i], w_in[i])
        s0 = jnp.zeros((bp, N_RWKV_HEADS, HEAD_DIM, HEAD_DIM), jnp.float32)
        o_r, S_p, sh_p = _rwkv_branch(f, jnp.zeros_like(f[:, :1]), s0, *rw)
        qh, kh, vh = _qkv(q, k, v, q_norm_w[i], k_norm_w[i])
        o_a = _swa_prompt(qh, kh, vh, sinks[i])
        wbp = min(WINDOW, xp.shape[1])
        kp_list.append(kh[:, -wbp:].astype(cache_k.dtype))
        vp_list.append(vh[:, -wbp:].astype(cache_v.dtype))
        s_p_list.append(S_p.astype(state_rwkv.dtype))
        sh_p_list.append(sh_p.astype(state_shift.dtype))
        xp = _merge(xp, o_r, z_r, o_a, z_a, w_out[i], p_prompt[i], g_ple[i], w_ple_gate[i], w_ple_proj[i])
        f, z_r, q, k, v, z_a = _in_proj(xs, g_norm[i], w_in[i])
        o_r, S_s, sh_s = _rwkv_branch(f, state_shift[i], state_rwkv[i], *rw)
        qh, kh, vh = _qkv(q, k, v, q_norm_w[i], k_norm_w[i])
        o_a, k_buf, v_buf = _swa_sample(qh, kh, vh, cache_k[i], cache_v[i], sinks[i])
        ks_list.append(k_buf.astype(cache_k.dtype))
        vs_list.append(v_buf.astype(cache_v.dtype))
        s_s_list.append(S_s.astype(state_rwkv.dtype))
        sh_s_list.append(sh_s.astype(state_shift.dtype))
        xs = _merge(xs, o_r, z_r, o_a, z_a, w_out[i], p_sample[i], g_ple[i], w_ple_gate[i], w_ple_proj[i])
    return (xp, xs,
            jnp.stack(s_p_list), jnp.stack(s_s_list),
            jnp.stack(sh_p_list), jnp.stack(sh_s_list),
            jnp.stack(kp_list), jnp.stack(ks_list),
            jnp.stack(vp_list), jnp.stack(vs_list))
```

```python
import numpy as np
from contextlib import ExitStack
import concourse.bass as bass
import concourse.mybir as mybir
from concourse.bass_utils import run_bass_kernel_spmd

F32 = mybir.dt.float32
BF16 = mybir.dt.bfloat16
I32 = mybir.dt.int32
AF = mybir.ActivationFunctionType
ALU = mybir.AluOpType

NCORES = 8
D = 1024
SEQ = 4096
NTILE = SEQ // 128
DIN = 3456
DSH = 1664
C_R, C_K, C_V, C_L, C_ZR, C_Q, C_KA, C_VA, C_ZA = 0, 512, 1024, 1536, 1664, 2176, 2688, 2816, 2944
NORM_EPS = 1e-6
LNX_EPS = 64e-5
DEC_C = -0.5 * float(np.exp(-0.5))


SAME_ENGINE_NOSYNC = ('pe', 'act', 'dve')


class Chan:
    def __init__(self, name, sem, step):
        self.name, self.sem, self.step, self.count = name, sem, step, 0


class Buf:
    def __init__(self, name, excl=False, always=False):
        self.name = name
        self.w = None
        self.r = {}
        self.excl = excl
        self.always = always


class MB:
    def __init__(self, name):
        self.parts = [Buf(name + "_g0"), Buf(name + "_g1")]


def _flat(bufs):
    out = []
    for b in bufs:
        if isinstance(b, MB):
            out.extend(b.parts)
        else:
            out.append(b)
    return out


class Same(MB):
    def __init__(self, b):
        self.parts = [b, b]


class Sched:
    def __init__(self, nc, es):
        self.nc = nc
        self.es = es
        self.eng = {'pe': nc.tensor, 'act': nc.scalar, 'dve': nc.vector, 'pool': nc.gpsimd, 'sp': nc.sync}
        self.chan = {k: Chan(k, es.enter_context(nc.semaphore(k)), 1) for k in ['pe', 'act', 'dve', 'pool']}
        self.seen = {q: {} for q in self.eng}
        self.dchans = []

    def dma_chan(self, name):
        c = Chan(name, self.es.enter_context(self.nc.semaphore(name)), 16)
        self.dchans.append(c)
        return c

    def op(self, q, fn, reads=(), writes=(), chan=None, sig=True):
        ch = chan or self.chan[q]
        reads = _flat(reads)
        writes = _flat(writes)
        deps = {}
        if not hasattr(self, 'pe_pend'):
            self.pe_pend = ([], [])

        def add(ev, force=False):
            if ev is None:
                return
            c, v = ev
            if c is ch and (c.name in SAME_ENGINE_NOSYNC or c.step == 16) and not (force and c.name != 'pe' and c.step == 1):
                return
            if v is None:
                raise RuntimeError("dependency on an unsignalled PE instruction")
            if c.name not in deps or deps[c.name][1] < v:
                deps[c.name] = (c, v)
        for b in reads:
            add(b.w, b.always)
            if b.excl:
                for ev in b.r.values():
                    if ev[0] is not ch:
                        add(ev)
        for b in writes:
            add(b.w, b.always)
            for ev in b.r.values():
                add(ev, b.always)
        for c, v in deps.values():
            if self.seen[q].get(c.name, 0) < v:
                self.eng[q].wait_ge(c.sem, v)
                self.seen[q][c.name] = v
        ins = fn(self.eng[q])
        if q == 'pe' and not sig:
            for b in reads:
                b.r[ch.name] = (ch, None)
                self.pe_pend[0].append(b)
            for b in writes:
                b.w = (ch, None)
                b.r = {}
                self.pe_pend[1].append(b)
            return ins
        ch.count += ch.step
        ins.then_inc(ch.sem, ch.step)
        ev = (ch, ch.count)
        if q == 'pe':
            for b in self.pe_pend[0]:
                if b.r.get(ch.name, (None, 0))[1] is None:
                    b.r[ch.name] = ev
            for b in self.pe_pend[1]:
                if b.w is not None and b.w[0] is ch and b.w[1] is None:
                    b.w = ev
            self.pe_pend = ([], [])
        for b in reads:
            b.r[ch.name] = ev
        for b in writes:
            b.w = ev
            b.r = {}
        return ins


def build_nc(debug=False, ntile=NTILE, do_sample=True, stop=None):
    nc = bass.Bass("TRN2", target_bir_lowering=False)
    dram_in = {}
    dram_out = {}

    def din(name, shape):
        dram_in[name] = nc.dram_tensor(name, list(shape), F32, kind="ExternalInput").ap()
        return dram_in[name]

    def dout(name, shape):
        dram_out[name] = nc.dram_tensor(name, list(shape), F32, kind="ExternalOutput").ap()
        return dram_out[name]

    x_p = din("x_p", [SEQ, D]); x_s = din("x_s", [128, D])
    st_r = din("st_r", [16, 8, 64, 64]); st_sh = din("st_sh", [16, DSH])
    c_k = din("c_k", [16, 128, 128]); c_v = din("c_v", [16, 128, 128])
    p_p = din("p_p", [SEQ, 256]); p_s = din("p_s", [128, 256])
    g_norm = din("g_norm", [D]); w_in = din("w_in", [D, DIN]); mu_shift = din("mu_shift", [DSH])
    w0 = din("w0", [512]); w_dec2 = din("w_dec2", [64, 512]); a0 = din("a0", [512]); w_a2 = din("w_a2", [64, 512])
    k_k = din("k_k", [512]); k_a = din("k_a", [512]); r_k = din("r_k", [512])
    lnx_w = din("lnx_w", [512]); lnx_b = din("lnx_b", [512])
    q_norm_w = din("q_norm_w", [64]); k_norm_w = din("k_norm_w", [64]); sinks = din("sinks", [8])
    w_out = din("w_out", [D, D]); g_ple = din("g_ple", [D]); w_gate = din("w_gate", [D, D]); w_ple = din("w_ple", [256, D])

    y_p = dout("y_p", [SEQ, D]); y_s = dout("y_s", [128, D])
    s_p = dout("s_p", [8, 64, 64]); s_s = dout("s_s", [16, 8, 64, 64])
    sh_p = dout("sh_p", [DSH]); sh_s = dout("sh_s", [16, DSH])
    ck_p = dout("ck_p", [128, 128]); ck_s = dout("ck_s", [16, 128, 128])
    cv_p = dout("cv_p", [128, 128]); cv_s = dout("cv_s", [16, 128, 128])
    dbg = {}

    es = ExitStack()
    with es:
        S = Sched(nc, es)

        def sb(name, shape, dt=F32):
            t = es.enter_context(nc.sbuf_tensor(name, list(shape), dt))
            return t, Buf(name)

        def TT(q, out, in0, in1, op, R, W):
            return S.op(q, lambda e: e.tensor_tensor(out=out, in0=in0, in1=in1, op=op), reads=R, writes=W)

        def TS(q, out, in0, s1, op0, R, W, s2=None, op1=None):
            if op1 is None:
                return S.op(q, lambda e: e.tensor_scalar(out=out, in0=in0, scalar1=s1, scalar2=None, op0=op0), reads=R, writes=W)
            return S.op(q, lambda e: e.tensor_scalar(out=out, in0=in0, scalar1=s1, scalar2=s2, op0=op0, op1=op1), reads=R, writes=W)

        def STT(out, in0, scalar, in1, op0, op1, R, W):
            return S.op('dve', lambda e: e.scalar_tensor_tensor(out=out, in0=in0, scalar=scalar, in1=in1, op0=op0, op1=op1), reads=R, writes=W)

        def ACT(out, in_, func, R, W, scale=1.0, bias=0.0, accum=None):
            if func == AF.Copy and not isinstance(scale, (int, float)):
                func = AF.Identity
            if accum is None:
                return S.op('act', lambda e: e.activation(out=out, in_=in_, func=func, scale=scale, bias=bias), reads=R, writes=W)
            return S.op('act', lambda e: e.activation(out=out, in_=in_, func=func, scale=scale, bias=bias, accum_out=accum), reads=R, writes=W)

        def CP(q, out, in_, R, W):
            if q == 'act':
                return ACT(out, in_, AF.Copy, R, W)
            return S.op(q, lambda e: e.tensor_copy(out=out, in_=in_), reads=R, writes=W)

        def MM(out, lhsT, rhs, R, W, start=True, stop=True, tp=None, sig=True):
            return S.op('pe', lambda e: e.matmul(out, lhsT=lhsT, rhs=rhs, start=start, stop=stop, tile_position=tp,
                                                 skip_group_check=True), reads=R, writes=W, sig=sig)

        def TR(out, in_, ident, R, W, sig=True):
            return S.op('pe', lambda e: e.transpose(out=out, in_=in_, identity=ident), reads=R, writes=W, sig=sig)

        def DMA(out, in_, R, W, chan, slow=False, q='sp'):
            return S.op(q, lambda e: e.dma_start(out=out, in_=in_, allow_slow_non_contiguous=slow), reads=R, writes=W, chan=chan)

        def POW(out, in_, R, W, n):
            if n == 1:
                return S.op('pool', lambda e: e.tensor_tensor(out=out, in0=in_, in1=mhalf[:, 0:n], op=ALU.pow), reads=R + [b_mhalf], writes=W)
            ACT(out, in_, AF.Sqrt, R, W)
            return S.op('dve', lambda e: e.reciprocal(out=out, in_=out), reads=W, writes=W)

        def RSQ(out, in_, scale, ecol, R, W):
            ACT(out, in_, AF.Ln, R + [b_epsc], W, scale=scale, bias=epsc[:, ecol:ecol + 1])
            ACT(out, out, AF.Exp, W, W, scale=-0.5)

        banks = []
        for i in range(8):
            t = es.enter_context(nc.psum_tensor(f"bank{i}", [128, 512], F32))
            banks.append((t, Buf(f"bank{i}", excl=True)))
        bank_ctr = [0]

        def pbank():
            t, b = banks[bank_ctr[0] % 7]
            bank_ctr[0] += 1
            return t, b

        pi_i, b_pi = sb("pi_i", [128, 128], I32); ji_i, b_ji = sb("ji_i", [128, 128], I32)
        S.op('pool', lambda e: e.iota(pi_i[:], pattern=[[0, 128]], base=0, channel_multiplier=1), writes=[b_pi])
        S.op('pool', lambda e: e.iota(ji_i[:], pattern=[[1, 128]], base=0, channel_multiplier=0), writes=[b_ji])
        tmp_i, b_tmpi = sb("tmp_i", [128, 128], I32)
        pf, b_pf = sb("pf", [128, 128]); jf, b_jf = sb("jf", [128, 128])
        CP('dve', pf[:], pi_i[:], [b_pi], [b_pf]); CP('dve', jf[:], ji_i[:], [b_ji], [b_jf])

        def shifted(name, src, bsrc, sh):
            t, b = sb(name, [128, 128])
            S.op('dve', lambda e: e.tensor_scalar(out=tmp_i[:], in0=src[:], scalar1=sh, scalar2=None, op0=ALU.arith_shift_right), reads=[bsrc], writes=[b_tmpi])
            CP('dve', t[:], tmp_i[:], [b_tmpi], [b])
            return t, b
        pc32, b_pc32 = shifted("pc32", pi_i, b_pi, 5); jc32, b_jc32 = shifted("jc32", ji_i, b_ji, 5)
        pc8, b_pc8 = shifted("pc8", pi_i, b_pi, 3); jc8, b_jc8 = shifted("jc8", ji_i, b_ji, 3)
        pc64, b_pc64 = shifted("pc64", pi_i, b_pi, 6); jc64, b_jc64 = shifted("jc64", ji_i, b_ji, 6)
        sc1, b_sc1 = sb("sc1", [128, 128]); sc2, b_sc2 = sb("sc2", [128, 128])

        def mk_mask(name, terms, dt=BF16):
            t, bt = sb(name, [128, 128], dt)
            a, ba, b_, bb, op = terms[0]
            if len(terms) == 1:
                TT('dve', t[:], a[:], b_[:], op, [ba, bb], [bt])
            else:
                TT('dve', sc1[:], a[:], b_[:], op, [ba, bb], [b_sc1])
                a, ba, b_, bb, op = terms[1]
                TT('dve', sc2[:], a[:], b_[:], op, [ba, bb], [b_sc2])
                TT('dve', t[:], sc1[:], sc2[:], ALU.mult, [b_sc1, b_sc2], [bt])
            return t, bt
        GT_, LE_ = (pf, b_pf, jf, b_jf, ALU.is_gt), (pf, b_pf, jf, b_jf, ALU.is_le)
        S32 = (pc32, b_pc32, jc32, b_jc32, ALU.is_equal); S8 = (pc8, b_pc8, jc8, b_jc8, ALU.is_equal)
        mL_p, b_mL_p = mk_mask("mL_p", [S32, GT_]); mU_p, b_mU_p = mk_mask("mU_p", [S32, LE_])
        mL_s, b_mL_s = mk_mask("mL_s", [S8, GT_]); mU_s, b_mU_s = mk_mask("mU_s", [S8, LE_])
        bd, b_bd = mk_mask("bd", [(pc64, b_pc64, jc64, b_jc64, ALU.is_equal)])
        mAU, b_mAU = mk_mask("mAU", [LE_]); mAL, b_mAL = mk_mask("mAL", [GT_])
        ident, b_ident = mk_mask("ident", [(pf, b_pf, jf, b_jf, ALU.is_equal)])
        identf, b_identf = mk_mask("identf", [(pf, b_pf, jf, b_jf, ALU.is_equal)], dt=F32)
        ones_b, b_ones = sb("ones_b", [128, 64], BF16)
        zeros_b, b_zeros = sb("zeros_b", [128, 128], BF16)
        S.op('pool', lambda e: e.memset(zeros_b[:], 0.0), writes=[b_zeros])
        S.op('pool', lambda e: e.memset(ones_b[:], 1.0), writes=[b_ones])
        cm_p, b_cm_p = sb("cm_p", [128, 4]); cm_s, b_cm_s = sb("cm_s", [128, 16])
        TT('dve', cm_p[:], pc32[:, 0:4], jf[:, 0:4], ALU.is_equal, [b_pc32, b_jf], [b_cm_p])
        TT('dve', cm_s[:], pc8[:, 0:16], jf[:, 0:16], ALU.is_equal, [b_pc8, b_jf], [b_cm_s])
        rm_p, b_rm_p = sb("rm_p", [128, 4, 128]); rm_s, b_rm_s = sb("rm_s", [128, 4, 128])
        for (rm, brm, jc, bjc, sh) in ((rm_p, b_rm_p, jc32, b_jc32, 32.0), (rm_s, b_rm_s, jc8, b_jc8, 8.0)):
            TS('dve', sc1[:], jc[:], sh, ALU.mult, [bjc], [b_sc1])
            TT('dve', sc2[:], jf[:], sc1[:], ALU.not_equal, [b_jf, b_sc1], [b_sc2])
            for c in range(4):
                CP('dve', rm[:, c, :], sc2[:], [b_sc2], [brm])

        fsA, b_fs = sb("fsA", [128, 13, 128])
        ld_small = S.dma_chan("ld_small")
        st_misc = S.dma_chan("st_misc")
        small_bufs = []

        colsT, b_colsT = sb("colsT", [64, 128])
        cols, b_cols = sb("cols", [128, 64])
        specs = [("mu", mu_shift, 13), ("w0", w0, 4), ("a0", a0, 4), ("kk", k_k, 4), ("ka", k_a, 4), ("rk", r_k, 4),
                 ("lnw", lnx_w, 4), ("lnb", lnx_b, 4), ("gn", g_norm, 8), ("gp", g_ple, 8)]
        coff = {}
        r0_ = 0
        for (nm_, vec_, n_) in specs:
            DMA(colsT[r0_:r0_ + n_, :], vec_.rearrange("(c p) -> c p", p=128), [], [b_colsT], ld_small)
            coff[nm_] = (r0_, n_)
            r0_ += n_
        small_bufs.append(b_colsT)

        def colv(nm_):
            o_, n_ = coff[nm_]
            return cols[:, o_:o_ + n_]
        mu_t, w0_t, a0_t, kk_t, ka_t, rk_t, lw_t, lb_t, gn_t, gp_t = [colv(k_) for k_ in ("mu", "w0", "a0", "kk", "ka", "rk", "lnw", "lnb", "gn", "gp")]
        b_mu = b_w0 = b_a0 = b_kkc = b_ka = b_rk = b_lnw = b_lnb = b_gn = b_gp = b_cols
        qw_t, b_qw = sb("qw", [128, 1]); kw_t, b_kw = sb("kw", [128, 1]); sk_t, b_sk = sb("sk", [128, 4])
        for kv in range(2):
            DMA(qw_t[64 * kv:64 * kv + 64, :], q_norm_w.rearrange("(p o) -> p o", o=1), [], [b_qw], ld_small, slow=True)
            DMA(kw_t[64 * kv:64 * kv + 64, :], k_norm_w.rearrange("(p o) -> p o", o=1), [], [b_kw], ld_small, slow=True)
            DMA(sk_t[64 * kv:64 * kv + 64, :], sinks[4 * kv:4 * kv + 4].partition_broadcast(64), [], [b_sk], ld_small, slow=True)
        small_bufs += [b_qw, b_kw, b_sk]
        wl_f, b_wlf = sb("wl_f", [128, 512])
        DMA(wl_f[0:64, :], w_dec2, [], [b_wlf], ld_small)
        DMA(wl_f[64:128, :], w_a2, [], [b_wlf], ld_small)
        small_bufs.append(b_wlf)
        if do_sample:
            shT, b_shT = sb("shT", [128, 13, 16])
            DMA(fsA[0:16, :, :], st_sh.rearrange("b (c p) -> b c p", p=128), [], [b_fs], ld_small)
            small_bufs.append(b_fs)
        for b in small_bufs:
            b.w = (ld_small, ld_small.count)
        bk, bb = pbank()
        TR(bk[:, 0:57], colsT[0:57, :], identf[0:57, 0:57], [b_colsT, b_identf], [bb])
        CP('dve', cols[:, 0:57], bk[:, 0:57], [bb], [b_cols])
        if do_sample:
            bk, bb = pbank()
            for j in range(13):
                TR(bk[:, j * 16:(j + 1) * 16], fsA[0:16, j, :], identf[0:16, 0:16], [b_fs, b_identf], [bb])
            CP('dve', shT[:].rearrange("p a b -> p (a b)"), bk[:, 0:208], [bb], [b_shT])

        mhalf, b_mhalf = sb("mhalf", [128, 1])
        epsc, b_epsc = sb("epsc", [128, 3])
        for j_, v_ in enumerate((NORM_EPS, 1e-24, LNX_EPS)):
            S.op('pool', lambda e, j_=j_, v_=v_: e.memset(epsc[:, j_:j_ + 1], v_), writes=[b_epsc])
        S.op('pool', lambda e: e.memset(mhalf[:], -0.5), writes=[b_mhalf])
        omu, b_omu = sb("omu", [128, 13]); hw0, b_hw0 = sb("hw0", [128, 4]); ha0, b_ha0 = sb("ha0", [128, 4])
        c1, b_c1 = sb("c1", [128, 4]); c2, b_c2 = sb("c2", [128, 4]); qw8, b_qw8 = sb("qw8", [128, 1]); esk, b_esk = sb("esk", [128, 4])
        TS('dve', omu[:], mu_t[:], -1.0, ALU.mult, [b_mu], [b_omu], s2=1.0, op1=ALU.add)
        TS('dve', hw0[:], w0_t[:], 0.5, ALU.mult, [b_w0], [b_hw0])
        TS('dve', ha0[:], a0_t[:], 0.5, ALU.mult, [b_a0], [b_ha0])
        TS('dve', c1[:], ka_t[:], -0.5, ALU.mult, [b_ka], [b_c1], s2=1.0, op1=ALU.add)
        TS('dve', c2[:], ka_t[:], 0.5, ALU.mult, [b_ka], [b_c2])
        TS('dve', qw8[:], qw_t[:], 0.125, ALU.mult, [b_qw], [b_qw8])
        ACT(esk[:], sk_t[:], AF.Exp, [b_sk], [b_esk])
        wl_b, b_wl = sb("wl_b", [128, 512], BF16)
        CP('dve', wl_b[:], wl_f[:], [b_wlf], [b_wl])

        Win, b_Win = sb("Win", [128, 8, DIN], BF16)
        Wout, b_Wout = sb("Wout", [128, 8, D], BF16)
        Wg, b_Wg = sb("Wg", [128, 8, D], BF16)
        Wp, b_Wp = sb("Wp", [128, 2, D], BF16)
        XS = [sb(f"xs{i}", [128, D]) for i in range(2)]
        XS_ld = [S.dma_chan(f"ld_x{i}") for i in range(2)]
        XS_st = [S.dma_chan(f"st_x{i}") for i in range(2)]
        SoutAll, _ = sb("SoutAll", [128, 2, 4, 128])
        b_SoutAll = MB("SoutAll")
        STG = [XS[0], XS[1], (fsA[:].rearrange("p a b -> p (a b)"), b_fs), (SoutAll[:].rearrange("p s a b -> p (s a b)"), b_SoutAll)]
        STG_ld = [XS_ld[0], XS_ld[1], S.dma_chan("ld_stg2"), S.dma_chan("ld_stg3")]
        stg_ctr = [0]

        def stage(src_ap, ncols):
            i = stg_ctr[0] % 4
            stg_ctr[0] += 1
            t, b = STG[i]
            if isinstance(src_ap, list):
                for (p0, p1, ap) in src_ap:
                    DMA(t[p0:p1, 0:ncols], ap, [], [b], STG_ld[i])
            else:
                DMA(t[:, 0:ncols], src_ap, [], [b], STG_ld[i])
            return t, b, stg_ctr[0] % 2
        qeng = ['act', 'dve']
        for kc in range(8):
            for blk in range(4):
                c0 = blk * 1024
                ncol = min(1024, DIN - c0)
                t, b, par = stage(w_in[kc * 128:(kc + 1) * 128, c0:c0 + ncol], ncol)
                q = qeng[par]
                segs = []
                lo, hi = c0, c0 + ncol
                for (s0, s1, perm) in ((0, C_Q, False), (C_Q, C_KA, True), (C_KA, C_ZA, False), (C_ZA, DIN, True)):
                    a, bnd = max(lo, s0), min(hi, s1)
                    if a < bnd:
                        segs.append((a, bnd, perm, s0))
                for (a, bnd, perm, s0) in segs:
                    if not perm:
                        if q == 'act':
                            ACT(Win[:, kc, a:bnd], t[:, a - c0:bnd - c0], AF.Copy, [b, b_gn], [b_Win], scale=gn_t[:, kc:kc + 1])
                        else:
                            TS('dve', Win[:, kc, a:bnd], t[:, a - c0:bnd - c0], gn_t[:, kc:kc + 1], ALU.mult, [b, b_gn], [b_Win])
                    else:
                        for piece in range((a - s0) // 64, (bnd - s0) // 64):
                            kv, g = piece // 4, piece % 4
                            so = s0 + piece * 64 - c0
                            do = s0 + g * 128 + kv * 64
                            TS('dve', Win[:, kc, do:do + 64], t[:, so:so + 64], gn_t[:, kc:kc + 1], ALU.mult, [b, b_gn], [b_Win])
        for kc in range(8):
            if kc < 4:
                src_ = w_out[kc * 128:(kc + 1) * 128, :]
            else:
                g = kc - 4
                src_ = [(64 * kv, 64 * kv + 64, w_out[512 + kv * 256 + g * 64: 512 + kv * 256 + g * 64 + 64, :]) for kv in range(2)]
            t, b, par = stage(src_, 1024)
            CP(qeng[par], Wout[:, kc, :], t[:, 0:1024], [b], [b_Wout])
        for kc in range(8):
            t, b, par = stage(w_gate[kc * 128:(kc + 1) * 128, :], 1024)
            if par == 0:
                ACT(Wg[:, kc, :], t[:, 0:1024], AF.Copy, [b, b_gp], [b_Wg], scale=gp_t[:, kc:kc + 1])
            else:
                TS('dve', Wg[:, kc, :], t[:, 0:1024], gp_t[:, kc:kc + 1], ALU.mult, [b, b_gp], [b_Wg])
        for kc in range(2):
            t, b, par = stage(w_ple[kc * 128:(kc + 1) * 128, :], 1024)
            CP(qeng[par], Wp[:, kc, :], t[:, 0:1024], [b], [b_Wp])

        PS = [sb(f"pt{i}", [128, 256]) for i in range(2)]
        PS_ld = [S.dma_chan(f"ld_p{i}") for i in range(2)]
        xn, b_xn = sb("xn", [128, D], BF16)
        Sbd0, b_Sbd0 = sb("Sbd0", [128, 4, 128]); Sbd1, b_Sbd1 = sb("Sbd1", [128, 4, 128])
        hn, b_hn = Sbd0[:].rearrange("p a b -> p (a b)").bitcast(BF16), b_Sbd0
        xnT, b_xnT = sb("xnT", [128, 8, 128], BF16)
        hnT, b_hnT = Sbd1[:].rearrange("p a b -> p (a b)").bitcast(BF16).rearrange("p (a b) -> p a b", b=128), b_Sbd1
        COLS = [[sb(f"col{a_}{b_}", [128, 1]) for b_ in range(3)] for a_ in range(2)]
        for cs_ in COLS:
            for (_, b_) in cs_:
                b_.always = True
        tmpS, b_tmpS = sb("tmpS", [128, 4, 129])
        carry, b_carry = sb("carry", [128, 13])
        S.op('pool', lambda e: e.memset(carry[:], 0.0), writes=[b_carry])
        fl, b_fl = sb("fl", [128, 13]); fls, b_fls = sb("fls", [128, 13, 16])
        lorain, b_lorain = sb("lorain", [128, 128], BF16)

        def f4(name, dt=F32):
            return sb(name, [128, 4, 128], dt)
        tA, b_tA = f4("tA"); tB, b_tB = f4("tB"); tC, b_tC = f4("tC"); tD, b_tD = f4("tD")
        b_tB = MB("tB")
        Pin, b_Pin = f4("Pin"); Pex, b_Pex = f4("Pex"); Piv, b_Piv = f4("Piv")
        kmod, b_kmod = f4("kmod"); bonus, b_bonus = f4("bonus")
        b_kmod = MB("kmod")
        H, b_H = f4("H")
        b_H = MB("H")
        scr = wl_f[:].rearrange("p (a b) -> p a b", b=128)
        b_scr = b_wlf
        tdec, b_tdec = tA, b_tA
        ta, b_ta = tB, b_tB
        cum, b_cum = tC, b_tC
        dif, b_dif = tD, b_tD
        kk, b_kk = tA, b_tA
        kkn, b_kkn = tC, b_tC
        km1, b_km1 = tD, b_tD
        alr, b_alr = tB, b_tB
        b2t, b_b2t = tA, b_tA
        PMc, b_PMc = tA, b_tA; HP, b_HP = tB, b_tB; tacc, b_tacc = tC, b_tC
        ysb, b_ysb = tD, b_tD; yc, b_yc = tA, b_tA; yo, b_yo = tB, b_tB
        dent, b_dent = tC, b_tC; oa, b_oa = tD, b_tD; dcs, b_dcs = tA, b_tA
        Hout, b_Hout = kmod, b_kmod
        rn, b_rn = scr, b_scr
        thz, b_thz = scr, b_scr
        sqb, b_sqb = f4("sqb", BF16)
        Rt, b_Rt = f4("Rt", BF16); At, b_At = f4("At", BF16); Bt, b_Bt = f4("Bt", BF16); Kt, b_Kt = f4("Kt", BF16)
        b_Rt = MB("Rt")
        vrk, b_vrk = f4("vrk", BF16)
        vbf, b_vbf = vrk, b_vrk; rkb, b_rkb = vrk, b_vrk; ybf, b_ybf = vrk, b_vrk
        Atm, b_Atm = f4("Atm", BF16); Btm, b_Btm = f4("Btm", BF16); Ktm, b_Ktm = f4("Ktm", BF16); Vtm, b_Vtm = f4("Vtm", BF16)
        b_Btm = MB("Btm"); b_Ktm = MB("Ktm")
        Khat, b_Khat = Ktm, b_Ktm
        RhT, b_RhT = Rt, b_Rt
        Zc, b_Zc = Btm, b_Btm
        Hbf, b_Hbf = f4("Hbf", BF16)
        b_Hbf = MB("Hbf")
        qT, b_qT = f4("qT", BF16)
        gr, b_gr = f4("gr", BF16); ga, b_ga = f4("ga", BF16)
        S.op('pool', lambda e: e.memset(H[:], 0.0), writes=[b_H])
        S.op('pool', lambda e: e.memset(Hbf[:], 0.0), writes=[b_Hbf])

        def a8(name, w=128):
            return sb(name, [128, 8, w], BF16)
        Aak, b_Aak = a8("Aak"); ArkT, b_ArkT = a8("ArkT")
        b_Aak = MB("Aak"); b_ArkT = MB("ArkT")
        AhT, b_AhT = ArkT, b_ArkT
        X, b_X = a8("X"); XT, b_XT = a8("XT"); Rh, b_Rh = a8("Rh", 192)
        b_X = MB("X"); b_XT = MB("XT"); b_Rh = MB("Rh")
        Vm, b_Vm = sb("Vm", [128, 4, 4, 128], BF16)
        GTm, b_GTm = sb("GTm", [128, 4, 4, 128], BF16)
        b_Vm = MB("Vm"); b_GTm = MB("GTm")

        def Zm(c):
            g = c // 2
            t, bt = (X, b_X) if c % 2 == 0 else (XT, b_XT)
            return t[:, 4 * g:4 * g + 4, :], bt.parts[g]
        Xf = X[:].rearrange("p a b -> p (a b)")
        XTf = XT[:].rearrange("p a b -> p (a b)")
        Es = [(Xf[:, 0:512], b_X.parts[0]), (Xf[:, 512:1024], b_X.parts[1]), (XTf[:, 0:512], b_XT.parts[0]), (XTf[:, 512:1024], b_XT.parts[1])]
        kTs = [sb(f"kT{i}", [128, 128], BF16) for i in range(2)]
        vtms = [sb(f"vtm{i}", [128, 128], BF16) for i in range(2)]
        kTf, b_kTf = pf, b_pf; vTf, b_vTf = jf, b_jf
        ktm_f, b_ktmf = sc1, b_sc1; vtm_f, b_vtmf = sc2, b_sc2
        rn1, b_rn1 = pc32, b_pc32
        sq1, b_sq1 = sqb[:, 0, :], b_sqb
        oT, b_oT = sb("oT", [128, 8, 128], BF16)
        pbf, b_pbf = sb("pbf", [128, 256], BF16); pT, b_pT = sb("pT", [128, 2, 128], BF16)
        tg = SoutAll[:].rearrange("p s a b -> p (s a b)")
        b_tg = b_SoutAll
        if do_sample:
            cvb, b_cvb = sb("cvb", [128, 16, 128], BF16)
            ckT, b_ckT = sb("ckT", [128, 16, 128], BF16)
            GTf = GTm[:].rearrange("p a b c -> p (a b c)")
            Ec = [(GTf[:, 0:512], b_GTm), (GTf[:, 512:1024], b_GTm)]
            Sbd = [(Sbd0, b_Sbd0), (Sbd1, b_Sbd1)]
            Sbd_ld = [S.dma_chan(f"ld_sbd{i}") for i in range(2)]
            Sout = [(SoutAll[:, 0], b_SoutAll.parts[0]), (SoutAll[:, 1], b_SoutAll.parts[1])]
            Sout_st = [S.dma_chan(f"st_so{i}") for i in range(2)]
            H0b, b_H0b = Hbf, b_Hbf
            ld_c = S.dma_chan("ld_c")

        def dbg_dump(name, ap_sb, bufs, shape):
            if not debug:
                return
            o = dout("dbg_" + name, shape)
            dbg[name] = o
            DMA(o, ap_sb, bufs, [], st_misc)

        def load_tile(i, samp):
            xs_t, xs_b = XS[i % 2]
            p_t, p_b = PS[i % 2]
            if samp:
                DMA(xs_t[:], x_s, [], [xs_b], XS_ld[i % 2])
                DMA(p_t[:], p_s, [], [p_b], PS_ld[i % 2])
            else:
                DMA(xs_t[:], x_p[i * 128:(i + 1) * 128, :], [], [xs_b], XS_ld[i % 2])
                DMA(p_t[:], p_p[i * 128:(i + 1) * 128, :], [], [p_b], PS_ld[i % 2])

        def rms_rows(src, bsrc, scratch, bscr, dst, bdst, cols=0):
            (c1_, b1_), (c2_, b2_), (c3_, b3_) = COLS[cols]
            ACT(scratch[:], src[:], AF.Square, [bsrc], [bscr, b1_], accum=c1_[:])
            TS('dve', c2_[:], c1_[:], 1.0 / D, ALU.mult, [b1_], [b2_], s2=NORM_EPS, op1=ALU.add)
            ACT(c3_[:], c2_[:], AF.Ln, [b2_], [b3_])
            ACT(c3_[:], c3_[:], AF.Exp, [b3_], [b3_], scale=-0.5)
            TS('dve', dst[:], src[:], c3_[:], ALU.mult, [bsrc, b3_], [bdst])

        def transpose8(src, bsrc, dst, bdst, n=8):
            bk, bb = pbank()
            bkb = bk[:].bitcast(BF16)
            for kc in range(n):
                TR(bkb[:, kc * 128:(kc + 1) * 128], src[:, kc * 128:(kc + 1) * 128], ident[:], [bsrc, b_ident], [bb], sig=(kc == n - 1))
            CP('act', dst[:].rearrange("p a b -> p (a b)"), bkb[:, 0:n * 128], [bb], [bdst])

        def inproj(bk, bb, slot, col0):
            for kc in range(8):
                MM(bk[:, slot * 128:(slot + 1) * 128], Win[:, kc, col0:col0 + 128], xnT[:, kc, :], [b_Win, b_xnT], [bb],
                   start=(kc == 0), stop=(kc == 7), sig=(kc == 7))

        def tile_prog(i, samp, first, last, part):
            xs_t, xs_b = XS[i % 2]
            p_t, p_b = PS[i % 2]
            par = i % 2
            mL, b_mL = (mL_s, b_mL_s) if samp else (mL_p, b_mL_p)
            mU, b_mU = (mU_s, b_mU_s) if samp else (mU_p, b_mU_p)
            rm, b_rm = (rm_s, b_rm_s) if samp else (rm_p, b_rm_p)
            if part == 'front':
                rms_rows(xs_t, xs_b, xn, b_xn, xn, b_xn)
                transpose8(xn, b_xn, xnT, b_xnT)
                for grp in ([12], [0, 1, 2, 3], [4, 5, 6, 7], [8, 9, 10, 11]):
                    bk, bb = pbank()
                    for s, j in enumerate(grp):
                        inproj(bk, bb, s, j * 128)
                    n = len(grp)
                    for s, j in enumerate(grp):
                        if s % 2 == 0:
                            ACT(tmpS[:, s, 1:129], bk[:, s * 128:(s + 1) * 128], AF.Copy, [bb, b_mu], [b_tmpS], scale=mu_t[:, j:j + 1])
                        else:
                            TS('dve', tmpS[:, s, 1:129], bk[:, s * 128:(s + 1) * 128], mu_t[:, j:j + 1], ALU.mult, [bb, b_mu], [b_tmpS])
                        if samp:
                            ACT(tmpS[:, s, 0:128:8], shT[:, j, :], AF.Copy, [b_shT, b_mu], [b_tmpS], scale=mu_t[:, j:j + 1])
                    if not samp:
                        CP('act', tmpS[:, 0:n, 0], carry[:, grp[0]:grp[0] + n], [b_carry], [b_tmpS])
                    for s, j in enumerate(grp):
                        STT(fsA[:, j, :], bk[:, s * 128:(s + 1) * 128], omu[:, j:j + 1], tmpS[:, s, 0:128], ALU.mult, ALU.add,
                            [bb, b_omu, b_tmpS], [b_fs])
                    if not samp:
                        CP('act', carry[:, grp[0]:grp[0] + n], tmpS[:, 0:n, 128], [b_tmpS], [b_carry])
                        if last:
                            CP('act', fl[:, grp[0]:grp[0] + n], bk[:, 0:n * 128].rearrange("p (a b) -> p a b", b=128)[:, :, 127], [bb], [b_fl])
                    else:
                        for s, j in enumerate(grp):
                            CP('act', fls[:, j, :], bk[:, s * 128 + 7:(s + 1) * 128:8], [bb], [b_fls])
                return
            if stop == 's2a':
                return
            if stop == 's2':
                return
            fl2 = lambda t: t[:].rearrange("p a b -> p (a b)")
            rF, kF, vF = fsA[:, 0:4, :], fsA[:, 4:8, :], fsA[:, 8:12, :]
            ACT(lorain[0:64, :], fsA[0:64, 12, :], AF.Tanh, [b_fs], [b_lorain])
            ACT(lorain[64:128, :], fsA[64:128, 12, :], AF.Copy, [b_fs], [b_lorain])
            bD, bbD = pbank()
            bA, bbA = pbank()
            for c in range(4):
                MM(bD[:, c * 128:(c + 1) * 128], wl_b[0:64, c * 128:(c + 1) * 128], lorain[0:64, :], [b_wl, b_lorain], [bbD])
            for c in range(4):
                MM(bA[:, c * 128:(c + 1) * 128], wl_b[64:128, c * 128:(c + 1) * 128], lorain[64:128, :], [b_wl, b_lorain], [bbA])
            for c in range(4):
                ACT(tdec[:, c, :], bD[:, c * 128:(c + 1) * 128], AF.Tanh, [bbD, b_hw0], [b_tdec], scale=0.5, bias=hw0[:, c:c + 1])
                ACT(ta[:, c, :], bA[:, c * 128:(c + 1) * 128], AF.Tanh, [bbA, b_ha0], [b_ta], scale=0.5, bias=ha0[:, c:c + 1])
            TS('dve', fl2(tdec), fl2(tdec), DEC_C, ALU.mult, [b_tdec], [b_tdec], s2=DEC_C, op1=ALU.add)
            S.op('dve', lambda e: e.tensor_tensor_scan(out=fl2(cum), data0=fl2(rm), data1=fl2(tdec), initial=0.0, op0=ALU.mult, op1=ALU.add),
                 reads=[b_rm, b_tdec], writes=[b_cum])
            TT('dve', fl2(dif), fl2(cum), fl2(tdec), ALU.subtract, [b_cum, b_tdec], [b_dif])
            ACT(fl2(Pin), fl2(cum), AF.Exp, [b_cum], [b_Pin])
            ACT(fl2(Piv), fl2(cum), AF.Exp, [b_cum], [b_Piv], scale=-1.0)
            ACT(fl2(Pex), fl2(dif), AF.Exp, [b_dif], [b_Pex])
            bk, bb = pbank()
            for s in range(4):
                inproj(bk, bb, s, C_ZR + s * 128)
            ACT(thz[:].rearrange("p a b -> p (a b)"), bk[:], AF.Tanh, [bb], [b_thz], scale=0.5)
            STT(gr[:].rearrange("p a b -> p (a b)"), thz[:].rearrange("p a b -> p (a b)"), 1.0, bk[:], ALU.add, ALU.mult, [b_thz, bb], [b_gr])
            bk, bb = pbank()
            for s in range(4):
                inproj(bk, bb, s, C_ZA + s * 128)
            ACT(thz[:].rearrange("p a b -> p (a b)"), bk[:], AF.Tanh, [bb], [b_thz], scale=0.5)
            STT(ga[:].rearrange("p a b -> p (a b)"), thz[:].rearrange("p a b -> p (a b)"), 1.0, bk[:], ALU.add, ALU.mult, [b_thz, bb], [b_ga])

            bq, bbq = pbank()
            for s in range(4):
                inproj(bq, bbq, s, C_Q + s * 128)
            ACT(sqb[:].rearrange("p a b -> p (a b)"), bq[:], AF.Square, [bbq], [b_sqb])
            bk, bb = pbank()
            MM(bk[:], bd[:], sqb[:].rearrange("p a b -> p (a b)"), [b_bd, b_sqb], [bb])
            RSQ(rn[:].rearrange("p a b -> p (a b)"), bk[:], 1.0 / 64, 0, [bb], [b_rn])
            STT(qT[:].rearrange("p a b -> p (a b)"), bq[:], qw8[:, 0:1], rn[:].rearrange("p a b -> p (a b)"), ALU.mult, ALU.mult,
                [bbq, b_qw8, b_rn], [b_qT])
            kT, b_kT = kTs[par]
            vtm, b_vtm = vtms[par]
            bkv, bbkv = pbank()
            inproj(bkv, bbkv, 0, C_KA)
            inproj(bkv, bbkv, 1, C_VA)
            ACT(sq1[:], bkv[:, 0:128], AF.Square, [bbkv], [b_sq1])
            bk, bb = pbank()
            MM(bk[:, 0:128], bd[:], sq1[:], [b_bd, b_sq1], [bb])
            RSQ(rn1[:], bk[:, 0:128], 1.0 / 64, 0, [bb], [b_rn1])
            STT(kTf[:], bkv[:, 0:128], kw_t[:, 0:1], rn1[:], ALU.mult, ALU.mult, [bbkv, b_kw, b_rn1], [b_kTf])
            CP('act', kT[:], kTf[:], [b_kTf], [b_kT])
            CP('act', vTf[:], bkv[:, 128:256], [bbkv], [b_vTf])
            bk, bb = pbank()
            TR(bk[:, 0:128], vTf[:], identf[:], [b_vTf, b_identf], [bb])
            CP('act', vtm_f[:], bk[:, 0:128], [bb], [b_vtmf])
            CP('dve', vtm[:], bk[:, 0:128], [bb], [b_vtm])
            if samp or last:
                bk, bb = pbank()
                TR(bk[:, 0:128], kTf[:], identf[:], [b_kTf, b_identf], [bb])
                CP('act', ktm_f[:], bk[:, 0:128], [bb], [b_ktmf])
            for c in range(4):
                TS('dve', kk[:, c, :], fsA[:, 4 + c, :], kk_t[:, c:c + 1], ALU.mult, [b_fs, b_kkc], [b_kk])
                TS('dve', km1[:, c, :], ta[:, c, :], c2[:, c:c + 1], ALU.mult, [b_ta, b_c2, b_c1], [b_km1], s2=c1[:, c:c + 1], op1=ALU.add)
            ACT(fl2(sqb), fl2(kk), AF.Square, [b_kk], [b_sqb])
            bk, bb = pbank()
            MM(bk[:], bd[:], fl2(sqb), [b_bd, b_sqb], [bb])
            RSQ(fl2(rn), bk[:], 1.0, 1, [bb], [b_rn])
            TT('dve', fl2(kkn), fl2(kk), fl2(rn), ALU.mult, [b_kk, b_rn], [b_kkn])
            TT('dve', kmod[:], kF, km1[:], ALU.mult, [b_fs, b_km1], [b_kmod])
            TT('dve', Rt[:], rF, Pin[:], ALU.mult, [b_fs, b_Pin], [b_Rt])
            STT(fl2(At), fl2(kkn), -1.0, fl2(Pex), ALU.mult, ALU.mult, [b_kkn, b_Pex], [b_At])
            TS('dve', fl2(alr), fl2(ta), 0.5, ALU.mult, [b_ta], [b_alr], s2=0.5, op1=ALU.add)
            TT('dve', fl2(b2t), fl2(kkn), fl2(alr), ALU.mult, [b_kkn, b_alr], [b_b2t])
            TT('dve', fl2(Bt), fl2(b2t), fl2(Piv), ALU.mult, [b_b2t, b_Piv], [b_Bt])
            TT('dve', fl2(Kt), fl2(kmod), fl2(Piv), ALU.mult, [b_kmod, b_Piv], [b_Kt])
            for c in range(4):
                STT(rkb[:, c, :], fsA[:, c, :], rk_t[:, c:c + 1], kmod[:, c, :], ALU.mult, ALU.mult, [b_fs, b_rk, b_kmod], [b_rkb])
            bk, bb = pbank()
            MM(bk[:], bd[:], fl2(rkb), [b_bd, b_rkb], [bb])
            TT('dve', bonus[:], bk[:].rearrange("p (a b) -> p a b", b=128), vF, ALU.mult, [bb, b_fs], [b_bonus])
            CP('act', vbf[:], vF, [b_fs], [b_vbf])
            for (srcs, dsts) in (((At, b_At, Atm, b_Atm), (Bt, b_Bt, Btm, b_Btm)), ((Kt, b_Kt, Ktm, b_Ktm), (vbf, b_vbf, Vtm, b_Vtm))):
                bk, bb = pbank()
                bkb = bk[:].bitcast(BF16)
                for n_, (src, bsrc, dst, bdst) in enumerate((srcs, dsts)):
                    for c in range(4):
                        TR(bkb[:, n_ * 512 + c * 128:n_ * 512 + (c + 1) * 128], src[:, c, :], ident[:], [bsrc, b_ident], [bb])
                for n_, (src, bsrc, dst, bdst) in enumerate((srcs, dsts)):
                    CP('act' if n_ == 0 else 'dve', fl2(dst), bkb[:, n_ * 512:(n_ + 1) * 512], [bb], [bdst])

            if stop == 's3':
                return
            nlev = 3 if samp else 5
            ngrp = 4 if samp else 1
            cm, b_cm = (cm_s, b_cm_s) if samp else (cm_p, b_cm_p)
            csz = 8 if samp else 32
            bY, bbY = banks[7]
            MM(bY[:], zeros_b[:], Win[:, 0, 0:512], [b_zeros, b_Win], [bbY], start=True, stop=True)
            G2 = (0, 1)

            def a_kind(g, L, bL, Rr, bR, dst, bdst, mask, bmask, lo=0, hi=128):
                for e in range(2):
                    bk, bb = pbank()
                    for cl in range(2):
                        c = 2 * g + cl
                        MM(bk[:, cl * 128:(cl + 1) * 128], L[64 * e:64 * e + 64, c, :], Rr[64 * e:64 * e + 64, c, :], [bL, bR], [bb])
                    TT('dve', dst[:, 4 * g + e:4 * g + 4:2, lo:hi], bk[:, 0:256].rearrange("p (a b) -> p a b", b=128),
                       mask[:].unsqueeze(1).broadcast_to([128, 2, 128]), ALU.mult, [bb, bmask], [bdst.parts[g]])
            for g in G2:
                a_kind(g, At, b_At, Bt, b_Bt, X, b_X, mL, b_mL)
                a_kind(g, Bt, b_Bt, Rt, b_Rt, Rh, b_Rh, mU, b_mU, 64, 192)
                CP('act', Rh[:, 4 * g:4 * g + 4, 0:64], Btm[:, 2 * g:2 * g + 2, :].rearrange("p c (e k) -> p (c e) k", k=64), [b_Btm.parts[g]], [b_Rh.parts[g]])
                S.op('dve', lambda e, g=g: e.transpose(out=XTf[:, 512 * g:512 * g + 512], in_=Xf[:, 512 * g:512 * g + 512]),
                     reads=[b_X.parts[g]], writes=[b_XT.parts[g]])
            for lev in range(nlev):
                if not samp and lev < 4:
                    c = lev
                    TT('dve', Vm[:, c], Vtm[:, c, :].unsqueeze(1).broadcast_to([128, 4, 128]), cm[:, 0:4].unsqueeze(2).broadcast_to([128, 4, 128]),
                       ALU.mult, [b_Vtm, b_cm], [b_Vm.parts[c // 2]])
                for g in G2:
                    if samp:
                        if lev == 1:
                            a_kind(g, At, b_At, Kt, b_Kt, Aak, b_Aak, mL, b_mL)
                            a_kind(g, Kt, b_Kt, Rt, b_Rt, ArkT, b_ArkT, mU, b_mU)
                    else:
                        if lev == 1 + g:
                            a_kind(g, At, b_At, Kt, b_Kt, Aak, b_Aak, mL, b_mL)
                        if lev == 2 + g:
                            a_kind(g, Kt, b_Kt, Rt, b_Rt, ArkT, b_ArkT, mU, b_mU)
                    bX, bXT, bR = b_X.parts[g], b_XT.parts[g], b_Rh.parts[g]
                    appb = []
                    for j in range(2):
                        bk, bb = pbank()
                        appb.append((bk, bb))
                        for hh in range(2):
                            h = 4 * g + 2 * j + hh
                            MM(bk[:, hh * 192:(hh + 1) * 192], X[:, h, :], Rh[:, h, :], [bX, bR], [bb], start=True, stop=(j == 1),
                               sig=(j == 1 and hh == 1))
                            if j == 0:
                                MM(bk[:, hh * 192:(hh + 1) * 192], ident[:], Rh[:, h, :], [b_ident, bR], [bb], start=False, stop=True, sig=(hh == 1))
                    if lev < nlev - 1:
                        bs, bbs = pbank()
                        for hh in range(4):
                            h = 4 * g + hh
                            MM(bs[:, hh * 128:(hh + 1) * 128], XT[:, h, :], X[:, h, :], [bXT, bX], [bbs], sig=(hh == 3))
                    for j in range(2):
                        bk, bb = appb[j]
                        h0 = 4 * g + 2 * j
                        if j == 0:
                            CP('act', Rh[:, h0:h0 + 2, :].rearrange("p a b -> p (a b)"), bk[:, 0:384], [bb], [bR])
                        else:
                            TT('dve', Rh[:, h0:h0 + 2, :], bk[:, 0:384].rearrange("p (a b) -> p a b", b=192), Rh[:, h0:h0 + 2, :], ALU.add,
                               [bb, bR], [bR])
                    if lev < nlev - 1:
                        CP('act', Xf[:, 512 * g:512 * g + 512], bs[:], [bbs], [bX])
                        S.op('dve', lambda e, g=g: e.transpose(out=XTf[:, 512 * g:512 * g + 512], in_=Xf[:, 512 * g:512 * g + 512]),
                             reads=[bX], writes=[bXT])
            for g in G2:
                bR = b_Rh.parts[g]
                for jl in range(2):
                    j = 2 * g + jl
                    bk, bb = pbank()
                    for hh in range(2):
                        h = 2 * j + hh
                        MM(bk[:, hh * 192:(hh + 1) * 192], Aak[:, h, :], Rh[:, h, :], [b_Aak.parts[g], bR], [bb])
                    v = bk[:, 0:384].rearrange("p (a b) -> p a b", b=192)
                    TT('dve', Khat[:, j, :].rearrange("p (e k) -> p e k", k=64), v[:, :, 0:64], Ktm[:, j, :].rearrange("p (e k) -> p e k", k=64), ALU.add,
                       [bb, b_Ktm.parts[g]], [b_Ktm.parts[g]])
                    TT('dve', AhT[:, 2 * j:2 * j + 2, :], v[:, :, 64:192], ArkT[:, 2 * j:2 * j + 2, :], ALU.add, [bb, b_ArkT.parts[g]], [b_ArkT.parts[g]])
                bk, bb = pbank()
                for hh in range(4):
                    h = 4 * g + hh
                    c, e = h // 2, h % 2
                    cl = c - 2 * g
                    MM(bk[64 * e:64 * e + 64, cl * 128:(cl + 1) * 128], Atm[:, c, 64 * e:64 * e + 64], Rh[:, h, 64:192], [b_Atm, bR], [bb],
                       tp=(0, 64 * e))
                TT('dve', RhT[:, 2 * g:2 * g + 2, :], bk[:, 0:256].rearrange("p (a b) -> p a b", b=128), Rt[:, 2 * g:2 * g + 2, :], ALU.add,
                   [bb, b_Rt.parts[g]], [b_Rt.parts[g]])
                CP('act', Zc[:, 2 * g:2 * g + 2, :].rearrange("p c (e k) -> p (c e) k", k=64), Rh[:, 4 * g:4 * g + 4, 0:64], [bR], [b_Btm.parts[g]])
            att = {}
            def attnA():
                blks = []
                if samp:
                    blks.append((kT, b_kT, vtm, b_vtm, mU_s, b_mU_s))
                else:
                    if not first:
                        kTp, b_kTp = kTs[1 - par]
                        vtp, b_vtp = vtms[1 - par]
                        blks.append((kTp, b_kTp, vtp, b_vtp, mAL, b_mAL))
                    blks.append((kT, b_kT, vtm, b_vtm, mAU, b_mAU))
                elist = []
                ei = 0
                for (kt_, bkt_, vt_, bvt_, mk_, bmk_) in blks:
                    for kv in range(2):
                        bk, bb = pbank()
                        MM(bk[:], kt_[64 * kv:64 * kv + 64, :], qT[64 * kv:64 * kv + 64, :, :], [bkt_, b_qT], [bb])
                        E_t, E_b = Es[ei]
                        ei += 1
                        ACT(E_t[:], bk[:], AF.Exp, [bb], [E_b])
                        TT('dve', E_t[:].rearrange("p (a b) -> p a b", b=128), E_t[:].rearrange("p (a b) -> p a b", b=128),
                           mk_[:].unsqueeze(1).broadcast_to([128, 4, 128]), ALU.mult, [E_b, bmk_], [E_b])
                        elist.append((kv, E_t, E_b, vt_, bvt_))
                att['elist'] = elist
            def attnB():
                elist = att['elist']
                bN, bbN = pbank()
                bDn, bbDn = pbank()
                if samp:
                    MM(bN[:], zeros_b[:], Win[:, 0, 0:512], [b_zeros, b_Win], [bbN], start=True, stop=True)
                for kv in range(2):
                    mine = [x for x in elist if x[0] == kv]
                    for n_, (_, E_t, E_b, vt_, bvt_) in enumerate(mine):
                        MM(bN[64 * kv:64 * kv + 64, :], vt_[:, 64 * kv:64 * kv + 64], E_t[:], [bvt_, E_b], [bbN], start=(n_ == 0 and not samp), stop=True, tp=(0, 64 * kv))
                        MM(bDn[64 * kv:64 * kv + 64, :], ones_b[:, 0:64], E_t[:], [b_ones, E_b], [bbDn], start=(n_ == 0), stop=True, tp=(0, 64 * kv))
                if samp:
                    bDc, bbDc = pbank()
                    for kv in range(2):
                        bk, bb = pbank()
                        for b in range(16):
                            MM(bk[:, b * 32:(b + 1) * 32].rearrange("p (g j) -> p g j", j=8), ckT[64 * kv:64 * kv + 64, b, :],
                               qT[64 * kv:64 * kv + 64, :, b * 8:(b + 1) * 8], [b_ckT, b_qT], [bb])
                        Ec_t, Ec_b = Ec[kv]
                        ACT(Ec_t[:], bk[:], AF.Exp, [bb], [Ec_b])
                        TT('dve', Ec_t[:].rearrange("p (a j) -> p a j", j=8), Ec_t[:].rearrange("p (a j) -> p a j", j=8),
                           mAL[:, 0:8].unsqueeze(1).broadcast_to([128, 64, 8]), ALU.mult, [Ec_b, b_mAL], [Ec_b])
                        for b in range(16):
                            MM(bN[64 * kv:64 * kv + 64, :].rearrange("p (g t) -> p g t", t=128)[:, :, b * 8:(b + 1) * 8],
                               cvb[:, b, 64 * kv:64 * kv + 64], Ec_t[:, b * 32:(b + 1) * 32].rearrange("p (g j) -> p g j", j=8),
                               [b_cvb, Ec_b], [bbN], start=False, stop=True, tp=(0, 64 * kv))
                        MM(bDc[64 * kv:64 * kv + 64, :], ones_b[:, 0:64], Ec_t[:], [b_ones, Ec_b], [bbDc], tp=(0, 64 * kv))
                    CP('act', dcs[:].rearrange("p g (b j) -> p g b j", j=8), bDc[:].rearrange("p (b g j) -> p g b j", g=4, j=8), [bbDc], [b_dcs])
                att['banks'] = (bN, bbN, bDn, bbDn) + ((bDc, bbDc) if samp else ())
            def attnC():
                bN, bbN, bDn, bbDn = att['banks'][0:4]
                TT('dve', dent[:], bDn[:].rearrange("p (a b) -> p a b", b=128), esk[:].unsqueeze(2).broadcast_to([128, 4, 128]), ALU.add,
                   [bbDn, b_esk], [b_dent])
                if samp:
                    TT('dve', fl2(dent), fl2(dent), fl2(dcs), ALU.add, [b_dent, b_dcs], [b_dent])
                ACT(fl2(dent), fl2(dent), AF.Ln, [b_dent], [b_dent])
                ACT(fl2(dent), fl2(dent), AF.Exp, [b_dent], [b_dent], scale=-1.0)
                TT('dve', fl2(oa), bN[:], fl2(dent), ALU.mult, [bbN, b_dent], [b_oa])
                if i == 0:
                    dbg_dump('oa', oa[:], [b_oa], [128, 4, 128])
                STT(oT[:, 4:8, :].rearrange("p a b -> p (a b)"), fl2(oa), 0.5, fl2(ga), ALU.mult, ALU.mult, [b_oa, b_ga], [b_oT])
            for g_ in range(ngrp):
                cmg = cm[:, g_ * 4:(g_ + 1) * 4]
                for g in G2:
                    for cl in range(2):
                        c = 2 * g + cl
                        zt, zb = Zm(c)
                        TT('dve', zt, Zc[:, c, :].unsqueeze(1).broadcast_to([128, 4, 128]), cmg.unsqueeze(2).broadcast_to([128, 4, 128]),
                           ALU.mult, [b_Btm.parts[g], b_cm], [zb])
                        if samp:
                            TT('dve', Vm[:, c], Vtm[:, c, :].unsqueeze(1).broadcast_to([128, 4, 128]), cmg.unsqueeze(2).broadcast_to([128, 4, 128]),
                               ALU.mult, [b_Vtm, b_cm], [b_Vm.parts[g]])
                    for cl in range(2):
                        c = 2 * g + cl
                        zt, zb = Zm(c)
                        bk, bb = pbank()
                        MM(bk[:], Atm[:, c, :], zt.rearrange("p a b -> p (a b)"), [b_Atm, zb], [bb])
                        TT('dve', GTm[:, c], bk[:].rearrange("p (a b) -> p a b", b=128), bd[:].unsqueeze(1).broadcast_to([128, 4, 128]), ALU.mult,
                           [bb, b_bd], [b_GTm.parts[g]])
                for cc in range(4):
                    ch_id = g_ * 4 + cc
                    t0 = ch_id * csz
                    te = t0 + csz - 1
                    if samp:
                        sl = ch_id % 2
                        sbd_t, sbd_b = Sbd[sl]
                        bk0, bb0 = pbank()
                        for c in range(4):
                            TR(bk0[:, c * 128:(c + 1) * 128], sbd_t[:, c, :], identf[:], [sbd_b, b_identf], [bb0])
                        H0b_s, bH0b_s = ((Hbf, b_Hbf), (sqb, Same(b_sqb)))[sl]
                        HP_s, bHP_s = ((tB, b_tB), (tA, Same(b_tA)))[sl]
                        Ho_s, bHo_s = ((kmod, b_kmod), (tC, Same(b_tC)))[sl]
                        CP('act', fl2(H0b_s), bk0[:], [bb0], [bH0b_s])
                        for g in G2:
                            TT('dve', HP_s[:, 2 * g:2 * g + 2, :], bk0[:, 256 * g:256 * g + 256].rearrange("p (a b) -> p a b", b=128),
                               Pin[:, 2 * g:2 * g + 2, te].unsqueeze(2).broadcast_to([128, 2, 128]), ALU.mult, [bb0, b_Pin], [bHP_s.parts[g]])
                        if ch_id + 2 < 16:
                            load_state(ch_id + 2)
                        Hs_b, bHb = H0b_s, bH0b_s
                        Hn, b_Hn = Ho_s, bHo_s
                        HPu, bHPu = HP_s, bHP_s
                    else:
                        for g in G2:
                            TT('dve', HP[:, 2 * g:2 * g + 2, :], H[:, 2 * g:2 * g + 2, :],
                               Pin[:, 2 * g:2 * g + 2, te].unsqueeze(2).broadcast_to([128, 2, 128]), ALU.mult, [b_H.parts[g], b_Pin], [b_HP.parts[g]])
                        Hs_b, bHb = Hbf, b_Hbf
                        Hn, b_Hn = H, b_H
                        HPu, bHPu = HP, b_HP
                    for g in G2:
                        bAcc, bbAcc = pbank()
                        for cl in range(2):
                            c = 2 * g + cl
                            MM(bY[:, c * 128 + t0:c * 128 + t0 + csz], Hs_b[:, c, :], RhT[:, c, t0:t0 + csz], [bHb.parts[g], b_Rt.parts[g]], [bbY],
                               start=False, stop=True, sig=False)
                            MM(bAcc[:, cl * 128:(cl + 1) * 128], GTm[:, c, cc, :], Hs_b[:, c, :], [b_GTm.parts[g], bHb.parts[g]], [bbAcc],
                               start=True, stop=False, sig=False)
                            for e in range(2):
                                MM(bAcc[64 * e:64 * e + 64, cl * 128 + 64 * e:cl * 128 + 64 * e + 64], Khat[:, c, 64 * e:64 * e + 64],
                                   Vm[:, c, cc, 64 * e:64 * e + 64], [b_Ktm.parts[g], b_Vm.parts[g]], [bbAcc], start=False, stop=True, tp=(0, 64 * e),
                                   sig=(cl == 1 and e == 1))
                        if not samp:
                            for cl in range(2):
                                c = 2 * g + cl
                                STT(Hbf[:, c, :], bAcc[:, cl * 128:(cl + 1) * 128], Pin[:, c, te:te + 1], HPu[:, c, :], ALU.mult, ALU.add,
                                    [bbAcc, b_Pin, bHPu.parts[g]], [b_Hbf.parts[g]])
                        for cl in range(2):
                            c = 2 * g + cl
                            STT(Hn[:, c, :], bAcc[:, cl * 128:(cl + 1) * 128], Pin[:, c, te:te + 1], HPu[:, c, :], ALU.mult, ALU.add,
                                [bbAcc, b_Pin, bHPu.parts[g]], [b_Hn.parts[g]])
                    if not samp and cc < 3:
                        (attnA, attnB, attnC)[cc]()
                    if samp:
                        bk, bb = pbank()
                        for c in range(4):
                            TR(bk[:, c * 128:(c + 1) * 128], Hn[:, c, :], identf[:], [b_Hn, b_identf], [bb])
                        so_t, so_b = Sout[sl]
                        CP('act', fl2(so_t), bk[:], [bb], [so_b])
                        for e in range(2):
                            DMA(s_s[ch_id].rearrange("(c e) v k -> e v c k", e=2)[e], so_t[64 * e:64 * e + 64, :, 64 * e:64 * e + 64],
                                [so_b], [], Sout_st[sl])
            if stop == 'chain':
                return
            for h in range(8):
                c, e = h // 2, h % 2
                MM(bY[64 * e:64 * e + 64, c * 128:(c + 1) * 128], Vtm[:, c, 64 * e:64 * e + 64], AhT[:, h, :], [b_Vtm, b_AhT], [bbY],
                   start=False, stop=True, tp=(0, 64 * e))
            if stop == 'p5':
                return
            CP('act', fl2(ysb), bY[:], [bbY], [b_ysb])
            CP('act', fl2(ybf), fl2(ysb), [b_ysb], [b_ybf])
            if i == 0:
                dbg_dump('ysb', ysb[:], [b_ysb], [128, 4, 128])
            bk, bb = pbank()
            MM(bk[:], bd[:], fl2(ybf), [b_bd, b_ybf], [bb])
            STT(fl2(yc), bk[:], -1.0 / 64, fl2(ysb), ALU.mult, ALU.add, [bb, b_ysb], [b_yc])
            ACT(fl2(sqb), fl2(yc), AF.Square, [b_yc], [b_sqb])
            bk, bb = pbank()
            MM(bk[:], bd[:], fl2(sqb), [b_bd, b_sqb], [bb])
            RSQ(fl2(rn), bk[:], 1.0 / 64, 2, [bb], [b_rn])
            TT('dve', fl2(yc), fl2(yc), fl2(rn), ALU.mult, [b_yc, b_rn], [b_yc])
            for c in range(4):
                TS('dve', yo[:, c, :], yc[:, c, :], lw_t[:, c:c + 1], ALU.mult, [b_yc, b_lnw, b_lnb], [b_yo], s2=lb_t[:, c:c + 1], op1=ALU.add)
            TT('dve', fl2(yo), fl2(yo), fl2(bonus), ALU.add, [b_yo, b_bonus], [b_yo])
            if i == 0:
                dbg_dump('yo', yo[:], [b_yo], [128, 4, 128])
            STT(oT[:, 0:4, :].rearrange("p a b -> p (a b)"), fl2(yo), 0.5, fl2(gr), ALU.mult, ALU.mult, [b_yo, b_gr], [b_oT])

            if stop == 's5':
                return
            if samp:
                attnA(); attnB(); attnC()
            return

        def tile_tail(i, samp):
            xs_t, xs_b = XS[i % 2]
            p_t, p_b = PS[i % 2]
            for n_ in range(2):
                bk, bb = pbank()
                for kc in range(8):
                    MM(bk[:], oT[:, kc, :], Wout[:, kc, n_ * 512:(n_ + 1) * 512], [b_oT, b_Wout], [bb], start=(kc == 0), stop=(kc == 7), sig=(kc == 7))
                TT('dve', xs_t[:, n_ * 512:(n_ + 1) * 512], bk[:], xs_t[:, n_ * 512:(n_ + 1) * 512], ALU.add, [bb, xs_b], [xs_b])
            if i == 0:
                dbg_dump('h', xs_t[:], [xs_b], [128, 1024])
            rms_rows(xs_t, xs_b, hn, b_hn, hn, b_hn, cols=1)
            transpose8(hn, b_hn, hnT, b_hnT)
            CP('act', pbf[:], p_t[:], [p_b], [b_pbf])
            bk, bb = pbank()
            bkb = bk[:].bitcast(BF16)
            for kc in range(2):
                TR(bkb[:, kc * 128:(kc + 1) * 128], pbf[:, kc * 128:(kc + 1) * 128], ident[:], [b_pbf, b_ident], [bb])
            CP('dve', pT[:].rearrange("p a b -> p (a b)"), bkb[:, 0:256], [bb], [b_pT])
            for n_ in range(2):
                bk, bb = pbank()
                for kc in range(8):
                    MM(bk[:], hnT[:, kc, :], Wg[:, kc, n_ * 512:(n_ + 1) * 512], [b_hnT, b_Wg], [bb], start=(kc == 0), stop=(kc == 7), sig=(kc == 7))
                ACT(tg[:, n_ * 512:(n_ + 1) * 512], bk[:], AF.Tanh, [bb], [b_tg], scale=0.5)
                TS('dve', tg[:, n_ * 512:(n_ + 1) * 512], tg[:, n_ * 512:(n_ + 1) * 512], 0.5, ALU.mult, [b_tg], [b_tg], s2=0.5, op1=ALU.add)
                bk2, bb2 = pbank()
                for kc in range(2):
                    MM(bk2[:], pT[:, kc, :], Wp[:, kc, n_ * 512:(n_ + 1) * 512], [b_pT, b_Wp], [bb2], start=(kc == 0), stop=(kc == 1))
                TT('dve', tg[:, n_ * 512:(n_ + 1) * 512], bk2[:], tg[:, n_ * 512:(n_ + 1) * 512], ALU.mult, [bb2, b_tg], [b_tg])
                TT('dve', xs_t[:, n_ * 512:(n_ + 1) * 512], xs_t[:, n_ * 512:(n_ + 1) * 512], tg[:, n_ * 512:(n_ + 1) * 512], ALU.add,
                   [xs_b, b_tg], [xs_b])
            if samp:
                DMA(y_s, xs_t[:], [xs_b], [], XS_st[i % 2])
            else:
                DMA(y_p[i * 128:(i + 1) * 128, :], xs_t[:], [xs_b], [], XS_st[i % 2])

        if do_sample:
            cst = SoutAll[:].rearrange("p s a b -> p (s a) b")
            for n_ in range(8):
                isk = n_ < 4
                src_c = (c_k if isk else c_v)
                b0 = (n_ % 4) * 4
                half = cst[:, (n_ % 2) * 4:(n_ % 2) * 4 + 4, :]
                DMA(half, src_c[b0:b0 + 4].rearrange("b i d -> i b d"), [], [b_SoutAll], ld_c)
                if isk:
                    cbs = xn[:, (n_ % 2) * 512:(n_ % 2) * 512 + 512]
                    CP('act', cbs.rearrange("p (a b) -> p a b", b=128), half, [b_SoutAll], [b_xn])
                    bk, bb = pbank()
                    bkb = bk[:].bitcast(BF16)
                    for b in range(4):
                        TR(bkb[:, b * 128:(b + 1) * 128], cbs[:, b * 128:(b + 1) * 128], ident[:], [b_xn, b_ident], [bb])
                    CP('act', ckT[:, b0:b0 + 4, :].rearrange("p a b -> p (a b)"), bkb[:, 0:512], [bb], [b_ckT])
                else:
                    CP('act', cvb[:, b0:b0 + 4, :], half, [b_SoutAll], [b_cvb])
        load_tile(0, False)
        if do_sample:
            st_cp = S.dma_chan("st_cp")
            DMA(ck_s[:, 0:120, :], c_k[:, 8:128, :], [], [], st_cp)
            DMA(cv_s[:, 0:120, :], c_v[:, 8:128, :], [], [], st_cp)
        tile_prog(0, False, True, ntile == 1, 'front')
        for i in range(0 if stop == 'w' else ntile):
            if i + 1 < ntile:
                load_tile(i + 1, False)
            elif do_sample:
                load_tile(ntile, True)
            tile_prog(i, False, i == 0, i == ntile - 1, 'mid')
            if i + 1 < ntile:
                tile_prog(i + 1, False, False, i + 1 == ntile - 1, 'front')
            elif do_sample:
                tile_prog(ntile, True, False, False, 'front')
            tile_tail(i, False)
        bk, bb = pbank()
        for c in range(4):
            TR(bk[:, c * 128:(c + 1) * 128], H[:, c, :], identf[:], [b_H, b_identf], [bb])
        CP('act', Hout[:].rearrange("p a b -> p (a b)"), bk[:], [bb], [b_Hout])
        for e in range(2):
            DMA(s_p.rearrange("(c e) v k -> e v c k", e=2)[e], Hout[64 * e:64 * e + 64, :, 64 * e:64 * e + 64], [b_Hout], [], st_misc)
        bk, bb = pbank()
        TR(bk[0:13, 0:128], fl[:, 0:13], identf[:], [b_fl, b_identf], [bb])
        CP('act', colsT[0:13, :], bk[0:13, 0:128], [bb], [b_colsT])
        DMA(sh_p.rearrange("(c p) -> c p", p=128), colsT[0:13, :], [b_colsT], [], st_misc)
        DMA(ck_p, ktm_f[:], [b_ktmf], [], st_misc)
        DMA(cv_p, vtm_f[:], [b_vtmf], [], st_misc)
        for b_ in _flat([b_Hout, b_colsT, b_ktmf, b_vtmf]):
            b_.r[st_misc.name] = (st_misc, st_misc.count)

        if do_sample:
            for sl in range(2):
                S.op('dve', lambda e, sl=sl: e.memset(Sbd[sl][0][:], 0.0), writes=[Sbd[sl][1]])

            def load_state(s_):
                sl = s_ % 2
                t_, b_ = Sbd[sl]
                for e in range(2):
                    DMA(t_[64 * e:64 * e + 64, :, 64 * e:64 * e + 64], st_r[s_].rearrange("(c e) v k -> e v c k", e=2)[e], [], [b_], Sbd_ld[sl])
            load_state(0)
            load_state(1)
            if stop != 'w':
                tile_prog(ntile, True, False, False, 'mid')
                tile_tail(ntile, True)
                for q4 in range(4):
                    bk, bb = pbank()
                    for j in range(q4 * 4, min(13, q4 * 4 + 4)):
                        TR(bk[0:16, (j % 4) * 128:(j % 4 + 1) * 128], fls[:, j, :], identf[:], [b_fls, b_identf], [bb])
                    nj = min(13, q4 * 4 + 4) - q4 * 4
                    CP('act' if q4 % 2 == 0 else 'dve', fsA[0:16, q4 * 4:q4 * 4 + nj, :].rearrange("p a b -> p (a b)"), bk[0:16, 0:nj * 128], [bb], [b_fs])
                DMA(sh_s.rearrange("b (c p) -> b c p", p=128), fsA[0:16, :, :], [b_fs], [], st_misc)
            for b in range(16):
                DMA(ck_s[b, 120:128, :], ktm_f[b * 8:(b + 1) * 8, :], [b_ktmf], [], st_misc)
                DMA(cv_s[b, 120:128, :], vtm_f[b * 8:(b + 1) * 8, :], [b_vtmf], [], st_misc)
        for ch in S.dchans:
            if ch.name.startswith("st_") and ch.count:
                nc.sync.wait_ge(ch.sem, ch.count)
    return nc, dram_out


_CACHE = {}


def kernel(**inp):
    f = lambda a: np.ascontiguousarray(np.asarray(a, dtype=np.float32))
    if "nc" not in _CACHE:
        _CACHE["nc"] = build_nc()
    nc, _ = _CACHE["nc"]
    shared = {
        "g_norm": f(inp["g_norm"][0]), "w_in": f(inp["w_in"][0]), "mu_shift": f(inp["mu_shift"][0]),
        "w0": f(inp["w0"][0]), "w_dec2": f(inp["w_dec2"][0]), "a0": f(inp["a0"][0]), "w_a2": f(inp["w_a2"][0]),
        "k_k": f(inp["k_k"][0]), "k_a": f(inp["k_a"][0]), "r_k": f(inp["r_k"][0]).reshape(512),
        "lnx_w": f(inp["lnx_w"][0]), "lnx_b": f(inp["lnx_b"][0]), "q_norm_w": f(inp["q_norm_w"][0]),
        "k_norm_w": f(inp["k_norm_w"][0]), "sinks": f(inp["sinks"][0]), "w_out": f(inp["w_out"][0]),
        "g_ple": f(inp["g_ple"][0]), "w_gate": f(inp["w_ple_gate"][0]), "w_ple": f(inp["w_ple_proj"][0]),
    }
    in_maps = []
    for c in range(NCORES):
        sl = slice(16 * c, 16 * c + 16)
        m = dict(shared)
        m["x_p"] = f(inp["x_prompt"][c]); m["x_s"] = f(inp["x_sample"][sl]).reshape(128, D)
        m["st_r"] = f(inp["state_rwkv"][0, sl]); m["st_sh"] = f(inp["state_shift"][0, sl, 0])
        m["c_k"] = f(inp["cache_k"][0, sl]).reshape(16, 128, 128); m["c_v"] = f(inp["cache_v"][0, sl]).reshape(16, 128, 128)
        m["p_p"] = f(inp["p_prompt"][0, c]); m["p_s"] = f(inp["p_sample"][0, sl]).reshape(128, 256)
        in_maps.append(m)
    res = run_bass_kernel_spmd(nc, in_maps, core_ids=list(range(NCORES)))
    R = res.results
    g = lambda k: [np.asarray(R[c][k], dtype=np.float32) for c in range(NCORES)]
    y_p = np.stack(g("y_p"))
    y_s = np.concatenate(g("y_s")).reshape(128, 8, D)
    s_p = np.stack(g("s_p"))[None]
    s_s = np.concatenate(g("s_s"))[None]
    sh_p = np.stack(g("sh_p")).reshape(1, 8, 1, DSH)
    sh_s = np.concatenate(g("sh_s")).reshape(1, 128, 1, DSH)
    ck_p = np.stack(g("ck_p")).reshape(1, 8, 128, 2, 64)
    ck_s = np.concatenate(g("ck_s")).reshape(1, 128, 128, 2, 64)
    cv_p = np.stack(g("cv_p")).reshape(1, 8, 128, 2, 64)
    cv_s = np.concatenate(g("cv_s")).reshape(1, 128, 128, 2, 64)
    return (y_p, y_s, s_p, s_s, sh_p, sh_s, ck_p, ck_s, cv_p, cv_s)
```

```python
import numpy as np
from contextlib import ExitStack
import concourse.bass as bass
import concourse.mybir as mybir
from concourse.bass_utils import run_bass_kernel_spmd

F32 = mybir.dt.float32
BF16 = mybir.dt.bfloat16
I32 = mybir.dt.int32
AF = mybir.ActivationFunctionType
ALU = mybir.AluOpType

NCORES = 8
D = 1024
SEQ = 4096
NTILE = SEQ // 128
DIN = 3456
DSH = 1664
C_R, C_K, C_V, C_L, C_ZR, C_Q, C_KA, C_VA, C_ZA = 0, 512, 1024, 1536, 1664, 2176, 2688, 2816, 2944
NORM_EPS = 1e-6
LNX_EPS = 64e-5
DEC_C = -0.5 * float(np.exp(-0.5))


SAME_ENGINE_NOSYNC = ('pe', 'act', 'dve')


class Chan:
    def __init__(self, name, sem, step):
        self.name, self.sem, self.step, self.count = name, sem, step, 0


class Buf:
    def __init__(self, name, excl=False, always=False):
        self.name = name
        self.w = None
        self.r = {}
        self.excl = excl
        self.always = always


class MB:
    def __init__(self, name):
        self.parts = [Buf(name + "_g0"), Buf(name + "_g1")]


def _flat(bufs):
    out = []
    for b in bufs:
        if isinstance(b, MB):
            out.extend(b.parts)
        else:
            out.append(b)
    return out


class Same(MB):
    def __init__(self, b):
        self.parts = [b, b]


class Sched:
    def __init__(self, nc, es):
        self.nc = nc
        self.es = es
        self.eng = {'pe': nc.tensor, 'act': nc.scalar, 'dve': nc.vector, 'pool': nc.gpsimd, 'sp': nc.sync}
        self.chan = {k: Chan(k, es.enter_context(nc.semaphore(k)), 1) for k in ['pe', 'act', 'dve', 'pool']}
        self.seen = {q: {} for q in self.eng}
        self.dchans = []

    def dma_chan(self, name):
        c = Chan(name, self.es.enter_context(self.nc.semaphore(name)), 16)
        self.dchans.append(c)
        return c

    def op(self, q, fn, reads=(), writes=(), chan=None, sig=True):
        ch = chan or self.chan[q]
        reads = _flat(reads)
        writes = _flat(writes)
        deps = {}
        if not hasattr(self, 'pe_pend'):
            self.pe_pend = ([], [])

        def add(ev, force=False):
            if ev is None:
                return
            c, v = ev
            if c is ch and (c.name in SAME_ENGINE_NOSYNC or c.step == 16) and not (force and c.name != 'pe' and c.step == 1):
                return
            if v is None:
                raise RuntimeError("dependency on an unsignalled PE instruction")
            if c.name not in deps or deps[c.name][1] < v:
                deps[c.name] = (c, v)
        for b in reads:
            add(b.w, b.always)
            if b.excl:
                for ev in b.r.values():
                    if ev[0] is not ch:
                        add(ev)
        for b in writes:
            add(b.w, b.always)
            for ev in b.r.values():
                add(ev, b.always)
        for c, v in deps.values():
            if self.seen[q].get(c.name, 0) < v:
                self.eng[q].wait_ge(c.sem, v)
                self.seen[q][c.name] = v
        ins = fn(self.eng[q])
        if q == 'pe' and not sig:
            for b in reads:
                b.r[ch.name] = (ch, None)
                self.pe_pend[0].append(b)
            for b in writes:
                b.w = (ch, None)
                b.r = {}
                self.pe_pend[1].append(b)
            return ins
        ch.count += ch.step
        ins.then_inc(ch.sem, ch.step)
        ev = (ch, ch.count)
        if q == 'pe':
            for b in self.pe_pend[0]:
                if b.r.get(ch.name, (None, 0))[1] is None:
                    b.r[ch.name] = ev
            for b in self.pe_pend[1]:
                if b.w is not None and b.w[0] is ch and b.w[1] is None:
                    b.w = ev
            self.pe_pend = ([], [])
        for b in reads:
            b.r[ch.name] = ev
        for b in writes:
            b.w = ev
            b.r = {}
        return ins


def build_nc(debug=False, ntile=NTILE, do_sample=True, stop=None):
    nc = bass.Bass("TRN2", target_bir_lowering=False)
    dram_in = {}
    dram_out = {}

    def din(name, shape):
        dram_in[name] = nc.dram_tensor(name, list(shape), F32, kind="ExternalInput").ap()
        return dram_in[name]

    def dout(name, shape):
        dram_out[name] = nc.dram_tensor(name, list(shape), F32, kind="ExternalOutput").ap()
        return dram_out[name]

    x_p = din("x_p", [SEQ, D]); x_s = din("x_s", [128, D])
    st_r = din("st_r", [16, 8, 64, 64]); st_sh = din("st_sh", [16, DSH])
    c_k = din("c_k", [16, 128, 128]); c_v = din("c_v", [16, 128, 128])
    p_p = din("p_p", [SEQ, 256]); p_s = din("p_s", [128, 256])
    g_norm = din("g_norm", [D]); w_in = din("w_in", [D, DIN]); mu_shift = din("mu_shift", [DSH])
    w0 = din("w0", [512]); w_dec2 = din("w_dec2", [64, 512]); a0 = din("a0", [512]); w_a2 = din("w_a2", [64, 512])
    k_k = din("k_k", [512]); k_a = din("k_a", [512]); r_k = din("r_k", [512])
    lnx_w = din("lnx_w", [512]); lnx_b = din("lnx_b", [512])
    q_norm_w = din("q_norm_w", [64]); k_norm_w = din("k_norm_w", [64]); sinks = din("sinks", [8])
    w_out = din("w_out", [D, D]); g_ple = din("g_ple", [D]); w_gate = din("w_gate", [D, D]); w_ple = din("w_ple", [256, D])

    y_p = dout("y_p", [SEQ, D]); y_s = dout("y_s", [128, D])
    s_p = dout("s_p", [8, 64, 64]); s_s = dout("s_s", [16, 8, 64, 64])
    sh_p = dout("sh_p", [DSH]); sh_s = dout("sh_s", [16, DSH])
    ck_p = dout("ck_p", [128, 128]); ck_s = dout("ck_s", [16, 128, 128])
    cv_p = dout("cv_p", [128, 128]); cv_s = dout("cv_s", [16, 128, 128])
    dbg = {}

    es = ExitStack()
    with es:
        S = Sched(nc, es)

        def sb(name, shape, dt=F32):
            t = es.enter_context(nc.sbuf_tensor(name, list(shape), dt))
            return t, Buf(name)

        def TT(q, out, in0, in1, op, R, W):
            return S.op(q, lambda e: e.tensor_tensor(out=out, in0=in0, in1=in1, op=op), reads=R, writes=W)

        def TS(q, out, in0, s1, op0, R, W, s2=None, op1=None):
            if op1 is None:
                return S.op(q, lambda e: e.tensor_scalar(out=out, in0=in0, scalar1=s1, scalar2=None, op0=op0), reads=R, writes=W)
            return S.op(q, lambda e: e.tensor_scalar(out=out, in0=in0, scalar1=s1, scalar2=s2, op0=op0, op1=op1), reads=R, writes=W)

        def STT(out, in0, scalar, in1, op0, op1, R, W):
            return S.op('dve', lambda e: e.scalar_tensor_tensor(out=out, in0=in0, scalar=scalar, in1=in1, op0=op0, op1=op1), reads=R, writes=W)

        def ACT(out, in_, func, R, W, scale=1.0, bias=0.0, accum=None):
            if func == AF.Copy and not isinstance(scale, (int, float)):
                func = AF.Identity
            if accum is None:
                return S.op('act', lambda e: e.activation(out=out, in_=in_, func=func, scale=scale, bias=bias), reads=R, writes=W)
            return S.op('act', lambda e: e.activation(out=out, in_=in_, func=func, scale=scale, bias=bias, accum_out=accum), reads=R, writes=W)

        def CP(q, out, in_, R, W):
            if q == 'act':
                return ACT(out, in_, AF.Copy, R, W)
            return S.op(q, lambda e: e.tensor_copy(out=out, in_=in_), reads=R, writes=W)

        def MM(out, lhsT, rhs, R, W, start=True, stop=True, tp=None, sig=True):
            return S.op('pe', lambda e: e.matmul(out, lhsT=lhsT, rhs=rhs, start=start, stop=stop, tile_position=tp,
                                                 skip_group_check=True), reads=R, writes=W, sig=sig)

        def TR(out, in_, ident, R, W, sig=True):
            return S.op('pe', lambda e: e.transpose(out=out, in_=in_, identity=ident), reads=R, writes=W, sig=sig)

        def DMA(out, in_, R, W, chan, slow=False, q='sp'):
            return S.op(q, lambda e: e.dma_start(out=out, in_=in_, allow_slow_non_contiguous=slow), reads=R, writes=W, chan=chan)

        def POW(out, in_, R, W, n):
            if n == 1:
                return S.op('pool', lambda e: e.tensor_tensor(out=out, in0=in_, in1=mhalf[:, 0:n], op=ALU.pow), reads=R + [b_mhalf], writes=W)
            ACT(out, in_, AF.Sqrt, R, W)
            return S.op('dve', lambda e: e.reciprocal(out=out, in_=out), reads=W, writes=W)

        def RSQ(out, in_, scale, ecol, R, W):
            ACT(out, in_, AF.Ln, R + [b_epsc], W, scale=scale, bias=epsc[:, ecol:ecol + 1])
            ACT(out, out, AF.Exp, W, W, scale=-0.5)

        banks = []
        for i in range(8):
            t = es.enter_context(nc.psum_tensor(f"bank{i}", [128, 512], F32))
            banks.append((t, Buf(f"bank{i}", excl=True)))
        bank_ctr = [0]

        def pbank():
            t, b = banks[bank_ctr[0] % 7]
            bank_ctr[0] += 1
            return t, b

        pi_i, b_pi = sb("pi_i", [128, 128], I32); ji_i, b_ji = sb("ji_i", [128, 128], I32)
        S.op('pool', lambda e: e.iota(pi_i[:], pattern=[[0, 128]], base=0, channel_multiplier=1), writes=[b_pi])
        S.op('pool', lambda e: e.iota(ji_i[:], pattern=[[1, 128]], base=0, channel_multiplier=0), writes=[b_ji])
        tmp_i, b_tmpi = sb("tmp_i", [128, 128], I32)
        pf, b_pf = sb("pf", [128, 128]); jf, b_jf = sb("jf", [128, 128])
        CP('dve', pf[:], pi_i[:], [b_pi], [b_pf]); CP('dve', jf[:], ji_i[:], [b_ji], [b_jf])

        def shifted(name, src, bsrc, sh):
            t, b = sb(name, [128, 128])
            S.op('dve', lambda e: e.tensor_scalar(out=tmp_i[:], in0=src[:], scalar1=sh, scalar2=None, op0=ALU.arith_shift_right), reads=[bsrc], writes=[b_tmpi])
            CP('dve', t[:], tmp_i[:], [b_tmpi], [b])
            return t, b
        pc32, b_pc32 = shifted("pc32", pi_i, b_pi, 5); jc32, b_jc32 = shifted("jc32", ji_i, b_ji, 5)
        pc8, b_pc8 = shifted("pc8", pi_i, b_pi, 3); jc8, b_jc8 = shifted("jc8", ji_i, b_ji, 3)
        pc64, b_pc64 = shifted("pc64", pi_i, b_pi, 6); jc64, b_jc64 = shifted("jc64", ji_i, b_ji, 6)
        sc1, b_sc1 = sb("sc1", [128, 128]); sc2, b_sc2 = sb("sc2", [128, 128])

        def mk_mask(name, terms, dt=BF16):
            t, bt = sb(name, [128, 128], dt)
            a, ba, b_, bb, op = terms[0]
            if len(terms) == 1:
                TT('dve', t[:], a[:], b_[:], op, [ba, bb], [bt])
            else:
                TT('dve', sc1[:], a[:], b_[:], op, [ba, bb], [b_sc1])
                a, ba, b_, bb, op = terms[1]
                TT('dve', sc2[:], a[:], b_[:], op, [ba, bb], [b_sc2])
                TT('dve', t[:], sc1[:], sc2[:], ALU.mult, [b_sc1, b_sc2], [bt])
            return t, bt
        GT_, LE_ = (pf, b_pf, jf, b_jf, ALU.is_gt), (pf, b_pf, jf, b_jf, ALU.is_le)
        S32 = (pc32, b_pc32, jc32, b_jc32, ALU.is_equal); S8 = (pc8, b_pc8, jc8, b_jc8, ALU.is_equal)
        mL_p, b_mL_p = mk_mask("mL_p", [S32, GT_]); mU_p, b_mU_p = mk_mask("mU_p", [S32, LE_])
        mL_s, b_mL_s = mk_mask("mL_s", [S8, GT_]); mU_s, b_mU_s = mk_mask("mU_s", [S8, LE_])
        bd, b_bd = mk_mask("bd", [(pc64, b_pc64, jc64, b_jc64, ALU.is_equal)])
        mAU, b_mAU = mk_mask("mAU", [LE_]); mAL, b_mAL = mk_mask("mAL", [GT_])
        ident, b_ident = mk_mask("ident", [(pf, b_pf, jf, b_jf, ALU.is_equal)])
        identf, b_identf = mk_mask("identf", [(pf, b_pf, jf, b_jf, ALU.is_equal)], dt=F32)
        ones_b, b_ones = sb("ones_b", [128, 64], BF16)
        zeros_b, b_zeros = sb("zeros_b", [128, 128], BF16)
        S.op('pool', lambda e: e.memset(zeros_b[:], 0.0), writes=[b_zeros])
        S.op('pool', lambda e: e.memset(ones_b[:], 1.0), writes=[b_ones])
        cm_p, b_cm_p = sb("cm_p", [128, 4]); cm_s, b_cm_s = sb("cm_s", [128, 16])
        TT('dve', cm_p[:], pc32[:, 0:4], jf[:, 0:4], ALU.is_equal, [b_pc32, b_jf], [b_cm_p])
        TT('dve', cm_s[:], pc8[:, 0:16], jf[:, 0:16], ALU.is_equal, [b_pc8, b_jf], [b_cm_s])
        rm_p, b_rm_p = sb("rm_p", [128, 4, 128]); rm_s, b_rm_s = sb("rm_s", [128, 4, 128])
        for (rm, brm, jc, bjc, sh) in ((rm_p, b_rm_p, jc32, b_jc32, 32.0), (rm_s, b_rm_s, jc8, b_jc8, 8.0)):
            TS('dve', sc1[:], jc[:], sh, ALU.mult, [bjc], [b_sc1])
            TT('dve', sc2[:], jf[:], sc1[:], ALU.not_equal, [b_jf, b_sc1], [b_sc2])
            for c in range(4):
                CP('dve', rm[:, c, :], sc2[:], [b_sc2], [brm])

        fsA, b_fs = sb("fsA", [128, 13, 128])
        ld_small = S.dma_chan("ld_small")
        st_misc = S.dma_chan("st_misc")
        small_bufs = []

        colsT, b_colsT = sb("colsT", [64, 128])
        cols, b_cols = sb("cols", [128, 64])
        specs = [("mu", mu_shift, 13), ("w0", w0, 4), ("a0", a0, 4), ("kk", k_k, 4), ("ka", k_a, 4), ("rk", r_k, 4),
                 ("lnw", lnx_w, 4), ("lnb", lnx_b, 4), ("gn", g_norm, 8), ("gp", g_ple, 8)]
        coff = {}
        r0_ = 0
        for (nm_, vec_, n_) in specs:
            DMA(colsT[r0_:r0_ + n_, :], vec_.rearrange("(c p) -> c p", p=128), [], [b_colsT], ld_small)
            coff[nm_] = (r0_, n_)
            r0_ += n_
        small_bufs.append(b_colsT)

        def colv(nm_):
            o_, n_ = coff[nm_]
            return cols[:, o_:o_ + n_]
        mu_t, w0_t, a0_t, kk_t, ka_t, rk_t, lw_t, lb_t, gn_t, gp_t = [colv(k_) for k_ in ("mu", "w0", "a0", "kk", "ka", "rk", "lnw", "lnb", "gn", "gp")]
        b_mu = b_w0 = b_a0 = b_kkc = b_ka = b_rk = b_lnw = b_lnb = b_gn = b_gp = b_cols
        qw_t, b_qw = sb("qw", [128, 1]); kw_t, b_kw = sb("kw", [128, 1]); sk_t, b_sk = sb("sk", [128, 4])
        for kv in range(2):
            DMA(qw_t[64 * kv:64 * kv + 64, :], q_norm_w.rearrange("(p o) -> p o", o=1), [], [b_qw], ld_small, slow=True)
            DMA(kw_t[64 * kv:64 * kv + 64, :], k_norm_w.rearrange("(p o) -> p o", o=1), [], [b_kw], ld_small, slow=True)
            DMA(sk_t[64 * kv:64 * kv + 64, :], sinks[4 * kv:4 * kv + 4].partition_broadcast(64), [], [b_sk], ld_small, slow=True)
        small_bufs += [b_qw, b_kw, b_sk]
        wl_f, b_wlf = sb("wl_f", [128, 512])
        DMA(wl_f[0:64, :], w_dec2, [], [b_wlf], ld_small)
        DMA(wl_f[64:128, :], w_a2, [], [b_wlf], ld_small)
        small_bufs.append(b_wlf)
        if do_sample:
            shT, b_shT = sb("shT", [128, 13, 16])
            DMA(fsA[0:16, :, :], st_sh.rearrange("b (c p) -> b c p", p=128), [], [b_fs], ld_small)
            small_bufs.append(b_fs)
        for b in small_bufs:
            b.w = (ld_small, ld_small.count)
        bk, bb = pbank()
        TR(bk[:, 0:57], colsT[0:57, :], identf[0:57, 0:57], [b_colsT, b_identf], [bb])
        CP('dve', cols[:, 0:57], bk[:, 0:57], [bb], [b_cols])
        if do_sample:
            bk, bb = pbank()
            for j in range(13):
                TR(bk[:, j * 16:(j + 1) * 16], fsA[0:16, j, :], identf[0:16, 0:16], [b_fs, b_identf], [bb])
            CP('dve', shT[:].rearrange("p a b -> p (a b)"), bk[:, 0:208], [bb], [b_shT])

        mhalf, b_mhalf = sb("mhalf", [128, 1])
        epsc, b_epsc = sb("epsc", [128, 3])
        for j_, v_ in enumerate((NORM_EPS, 1e-24, LNX_EPS)):
            S.op('pool', lambda e, j_=j_, v_=v_: e.memset(epsc[:, j_:j_ + 1], v_), writes=[b_epsc])
        S.op('pool', lambda e: e.memset(mhalf[:], -0.5), writes=[b_mhalf])
        omu, b_omu = sb("omu", [128, 13]); hw0, b_hw0 = sb("hw0", [128, 4]); ha0, b_ha0 = sb("ha0", [128, 4])
        c1, b_c1 = sb("c1", [128, 4]); c2, b_c2 = sb("c2", [128, 4]); qw8, b_qw8 = sb("qw8", [128, 1]); esk, b_esk = sb("esk", [128, 4])
        TS('dve', omu[:], mu_t[:], -1.0, ALU.mult, [b_mu], [b_omu], s2=1.0, op1=ALU.add)
        TS('dve', hw0[:], w0_t[:], 0.5, ALU.mult, [b_w0], [b_hw0])
        TS('dve', ha0[:], a0_t[:], 0.5, ALU.mult, [b_a0], [b_ha0])
        TS('dve', c1[:], ka_t[:], -0.5, ALU.mult, [b_ka], [b_c1], s2=1.0, op1=ALU.add)
        TS('dve', c2[:], ka_t[:], 0.5, ALU.mult, [b_ka], [b_c2])
        TS('dve', qw8[:], qw_t[:], 0.125, ALU.mult, [b_qw], [b_qw8])
        ACT(esk[:], sk_t[:], AF.Exp, [b_sk], [b_esk])
        wl_b, b_wl = sb("wl_b", [128, 512], BF16)
        CP('dve', wl_b[:], wl_f[:], [b_wlf], [b_wl])

        Win, b_Win = sb("Win", [128, 8, DIN], BF16)
        Wout, b_Wout = sb("Wout", [128, 8, D], BF16)
        Wg, b_Wg = sb("Wg", [128, 8, D], BF16)
        Wp, b_Wp = sb("Wp", [128, 2, D], BF16)
        XS = [sb(f"xs{i}", [128, D]) for i in range(2)]
        XS_ld = [S.dma_chan(f"ld_x{i}") for i in range(2)]
        XS_st = [S.dma_chan(f"st_x{i}") for i in range(2)]
        SoutAll, _ = sb("SoutAll", [128, 2, 4, 128])
        b_SoutAll = MB("SoutAll")
        STG = [XS[0], XS[1], (fsA[:].rearrange("p a b -> p (a b)"), b_fs), (SoutAll[:].rearrange("p s a b -> p (s a b)"), b_SoutAll)]
        STG_ld = [XS_ld[0], XS_ld[1], S.dma_chan("ld_stg2"), S.dma_chan("ld_stg3")]
        stg_ctr = [0]

        def stage(src_ap, ncols):
            i = stg_ctr[0] % 4
            stg_ctr[0] += 1
            t, b = STG[i]
            if isinstance(src_ap, list):
                for (p0, p1, ap) in src_ap:
                    DMA(t[p0:p1, 0:ncols], ap, [], [b], STG_ld[i])
            else:
                DMA(t[:, 0:ncols], src_ap, [], [b], STG_ld[i])
            return t, b, stg_ctr[0] % 2
        qeng = ['act', 'dve']
        for kc in range(8):
            for blk in range(4):
                c0 = blk * 1024
                ncol = min(1024, DIN - c0)
                t, b, par = stage(w_in[kc * 128:(kc + 1) * 128, c0:c0 + ncol], ncol)
                q = qeng[par]
                segs = []
                lo, hi = c0, c0 + ncol
                for (s0, s1, perm) in ((0, C_Q, False), (C_Q, C_KA, True), (C_KA, C_ZA, False), (C_ZA, DIN, True)):
                    a, bnd = max(lo, s0), min(hi, s1)
                    if a < bnd:
                        segs.append((a, bnd, perm, s0))
                for (a, bnd, perm, s0) in segs:
                    if not perm:
                        if q == 'act':
                            ACT(Win[:, kc, a:bnd], t[:, a - c0:bnd - c0], AF.Copy, [b, b_gn], [b_Win], scale=gn_t[:, kc:kc + 1])
                        else:
                            TS('dve', Win[:, kc, a:bnd], t[:, a - c0:bnd - c0], gn_t[:, kc:kc + 1], ALU.mult, [b, b_gn], [b_Win])
                    else:
                        for piece in range((a - s0) // 64, (bnd - s0) // 64):
                            kv, g = piece // 4, piece % 4
                            so = s0 + piece * 64 - c0
                            do = s0 + g * 128 + kv * 64
                            TS('dve', Win[:, kc, do:do + 64], t[:, so:so + 64], gn_t[:, kc:kc + 1], ALU.mult, [b, b_gn], [b_Win])
        for kc in range(8):
            if kc < 4:
                src_ = w_out[kc * 128:(kc + 1) * 128, :]
            else:
                g = kc - 4
                src_ = [(64 * kv, 64 * kv + 64, w_out[512 + kv * 256 + g * 64: 512 + kv * 256 + g * 64 + 64, :]) for kv in range(2)]
            t, b, par = stage(src_, 1024)
            CP(qeng[par], Wout[:, kc, :], t[:, 0:1024], [b], [b_Wout])
        for kc in range(8):
            t, b, par = stage(w_gate[kc * 128:(kc + 1) * 128, :], 1024)
            if par == 0:
                ACT(Wg[:, kc, :], t[:, 0:1024], AF.Copy, [b, b_gp], [b_Wg], scale=gp_t[:, kc:kc + 1])
            else:
                TS('dve', Wg[:, kc, :], t[:, 0:1024], gp_t[:, kc:kc + 1], ALU.mult, [b, b_gp], [b_Wg])
        for kc in range(2):
            t, b, par = stage(w_ple[kc * 128:(kc + 1) * 128, :], 1024)
            CP(qeng[par], Wp[:, kc, :], t[:, 0:1024], [b], [b_Wp])

        PS = [sb(f"pt{i}", [128, 256]) for i in range(2)]
        PS_ld = [S.dma_chan(f"ld_p{i}") for i in range(2)]
        xn, b_xn = sb("xn", [128, D], BF16)
        Sbd0, b_Sbd0 = sb("Sbd0", [128, 4, 128]); Sbd1, b_Sbd1 = sb("Sbd1", [128, 4, 128])
        hn, b_hn = Sbd0[:].rearrange("p a b -> p (a b)").bitcast(BF16), b_Sbd0
        xnT, b_xnT = sb("xnT", [128, 8, 128], BF16)
        hnT, b_hnT = Sbd1[:].rearrange("p a b -> p (a b)").bitcast(BF16).rearrange("p (a b) -> p a b", b=128), b_Sbd1
        COLS = [[sb(f"col{a_}{b_}", [128, 1]) for b_ in range(3)] for a_ in range(2)]
        for cs_ in COLS:
            for (_, b_) in cs_:
                b_.always = True
        tmpS, b_tmpS = sb("tmpS", [128, 4, 129])
        carry, b_carry = sb("carry", [128, 13])
        S.op('pool', lambda e: e.memset(carry[:], 0.0), writes=[b_carry])
        fl, b_fl = sb("fl", [128, 13]); fls, b_fls = sb("fls", [128, 13, 16])
        lorain, b_lorain = sb("lorain", [128, 128], BF16)

        def f4(name, dt=F32):
            return sb(name, [128, 4, 128], dt)
        tA, b_tA = f4("tA"); tB, b_tB = f4("tB"); tC, b_tC = f4("tC"); tD, b_tD = f4("tD")
        b_tB = MB("tB")
        Pin, b_Pin = f4("Pin"); Pex, b_Pex = f4("Pex"); Piv, b_Piv = f4("Piv")
        kmod, b_kmod = f4("kmod"); bonus, b_bonus = f4("bonus")
        b_kmod = MB("kmod")
        H, b_H = f4("H")
        b_H = MB("H")
        scr = wl_f[:].rearrange("p (a b) -> p a b", b=128)
        b_scr = b_wlf
        tdec, b_tdec = tA, b_tA
        ta, b_ta = tB, b_tB
        cum, b_cum = tC, b_tC
        dif, b_dif = tD, b_tD
        kk, b_kk = tA, b_tA
        kkn, b_kkn = tC, b_tC
        km1, b_km1 = tD, b_tD
        alr, b_alr = tB, b_tB
        b2t, b_b2t = tA, b_tA
        PMc, b_PMc = tA, b_tA; HP, b_HP = tB, b_tB; tacc, b_tacc = tC, b_tC
        ysb, b_ysb = tD, b_tD; yc, b_yc = tA, b_tA; yo, b_yo = tB, b_tB
        dent, b_dent = tC, b_tC; oa, b_oa = tD, b_tD; dcs, b_dcs = tA, b_tA
        Hout, b_Hout = kmod, b_kmod
        rn, b_rn = scr, b_scr
        thz, b_thz = scr, b_scr
        sqb, b_sqb = f4("sqb", BF16)
        Rt, b_Rt = f4("Rt", BF16); At, b_At = f4("At", BF16); Bt, b_Bt = f4("Bt", BF16); Kt, b_Kt = f4("Kt", BF16)
        b_Rt = MB("Rt")
        vrk, b_vrk = f4("vrk", BF16)
        vbf, b_vbf = vrk, b_vrk; rkb, b_rkb = vrk, b_vrk; ybf, b_ybf = vrk, b_vrk
        Atm, b_Atm = f4("Atm", BF16); Btm, b_Btm = f4("Btm", BF16); Ktm, b_Ktm = f4("Ktm", BF16); Vtm, b_Vtm = f4("Vtm", BF16)
        b_Btm = MB("Btm"); b_Ktm = MB("Ktm")
        Khat, b_Khat = Ktm, b_Ktm
        RhT, b_RhT = Rt, b_Rt
        Zc, b_Zc = Btm, b_Btm
        Hbf, b_Hbf = f4("Hbf", BF16)
        b_Hbf = MB("Hbf")
        qT, b_qT = f4("qT", BF16)
        gr, b_gr = f4("gr", BF16); ga, b_ga = f4("ga", BF16)
        S.op('pool', lambda e: e.memset(H[:], 0.0), writes=[b_H])
        S.op('pool', lambda e: e.memset(Hbf[:], 0.0), writes=[b_Hbf])

        def a8(name, w=128):
            return sb(name, [128, 8, w], BF16)
        Aak, b_Aak = a8("Aak"); ArkT, b_ArkT = a8("ArkT")
        b_Aak = MB("Aak"); b_ArkT = MB("ArkT")
        AhT, b_AhT = ArkT, b_ArkT
        X, b_X = a8("X"); XT, b_XT = a8("XT"); Rh, b_Rh = a8("Rh", 192)
        b_X = MB("X"); b_XT = MB("XT"); b_Rh = MB("Rh")
        Vm, b_Vm = sb("Vm", [128, 4, 4, 128], BF16)
        GTm, b_GTm = sb("GTm", [128, 4, 4, 128], BF16)
        b_Vm = MB("Vm"); b_GTm = MB("GTm")

        def Zm(c):
            g = c // 2
            t, bt = (X, b_X) if c % 2 == 0 else (XT, b_XT)
            return t[:, 4 * g:4 * g + 4, :], bt.parts[g]
        Xf = X[:].rearrange("p a b -> p (a b)")
        XTf = XT[:].rearrange("p a b -> p (a b)")
        Es = [(Xf[:, 0:512], b_X.parts[0]), (Xf[:, 512:1024], b_X.parts[1]), (XTf[:, 0:512], b_XT.parts[0]), (XTf[:, 512:1024], b_XT.parts[1])]
        kTs = [sb(f"kT{i}", [128, 128], BF16) for i in range(2)]
        vtms = [sb(f"vtm{i}", [128, 128], BF16) for i in range(2)]
        kTf, b_kTf = pf, b_pf; vTf, b_vTf = jf, b_jf
        ktm_f, b_ktmf = sc1, b_sc1; vtm_f, b_vtmf = sc2, b_sc2
        rn1, b_rn1 = pc32, b_pc32
        sq1, b_sq1 = sqb[:, 0, :], b_sqb
        oT, b_oT = sb("oT", [128, 8, 128], BF16)
        pbf, b_pbf = sb("pbf", [128, 256], BF16); pT, b_pT = sb("pT", [128, 2, 128], BF16)
        tg = SoutAll[:].rearrange("p s a b -> p (s a b)")
        b_tg = b_SoutAll
        if do_sample:
            cvb, b_cvb = sb("cvb", [128, 16, 128], BF16)
            ckT, b_ckT = sb("ckT", [128, 16, 128], BF16)
            GTf = GTm[:].rearrange("p a b c -> p (a b c)")
            Ec = [(GTf[:, 0:512], b_GTm), (GTf[:, 512:1024], b_GTm)]
            Sbd = [(Sbd0, b_Sbd0), (Sbd1, b_Sbd1)]
            Sbd_ld = [S.dma_chan(f"ld_sbd{i}") for i in range(2)]
            Sout = [(SoutAll[:, 0], b_SoutAll.parts[0]), (SoutAll[:, 1], b_SoutAll.parts[1])]
            Sout_st = [S.dma_chan(f"st_so{i}") for i in range(2)]
            H0b, b_H0b = Hbf, b_Hbf
            ld_c = S.dma_chan("ld_c")

        def dbg_dump(name, ap_sb, bufs, shape):
            if not debug:
                return
            o = dout("dbg_" + name, shape)
            dbg[name] = o
            DMA(o, ap_sb, bufs, [], st_misc)

        def load_tile(i, samp):
            xs_t, xs_b = XS[i % 2]
            p_t, p_b = PS[i % 2]
            if samp:
                DMA(xs_t[:], x_s, [], [xs_b], XS_ld[i % 2])
                DMA(p_t[:], p_s, [], [p_b], PS_ld[i % 2])
            else:
                DMA(xs_t[:], x_p[i * 128:(i + 1) * 128, :], [], [xs_b], XS_ld[i % 2])
                DMA(p_t[:], p_p[i * 128:(i + 1) * 128, :], [], [p_b], PS_ld[i % 2])

        def rms_rows(src, bsrc, scratch, bscr, dst, bdst, cols=0):
            (c1_, b1_), (c2_, b2_), (c3_, b3_) = COLS[cols]
            ACT(scratch[:], src[:], AF.Square, [bsrc], [bscr, b1_], accum=c1_[:])
            TS('dve', c2_[:], c1_[:], 1.0 / D, ALU.mult, [b1_], [b2_], s2=NORM_EPS, op1=ALU.add)
            ACT(c3_[:], c2_[:], AF.Ln, [b2_], [b3_])
            ACT(c3_[:], c3_[:], AF.Exp, [b3_], [b3_], scale=-0.5)
            TS('dve', dst[:], src[:], c3_[:], ALU.mult, [bsrc, b3_], [bdst])

        def transpose8(src, bsrc, dst, bdst, n=8):
            bk, bb = pbank()
            bkb = bk[:].bitcast(BF16)
            for kc in range(n):
                TR(bkb[:, kc * 128:(kc + 1) * 128], src[:, kc * 128:(kc + 1) * 128], ident[:], [bsrc, b_ident], [bb], sig=(kc == n - 1))
            CP('act', dst[:].rearrange("p a b -> p (a b)"), bkb[:, 0:n * 128], [bb], [bdst])

        def inproj(bk, bb, slot, col0):
            for kc in range(8):
                MM(bk[:, slot * 128:(slot + 1) * 128], Win[:, kc, col0:col0 + 128], xnT[:, kc, :], [b_Win, b_xnT], [bb],
                   start=(kc == 0), stop=(kc == 7), sig=(kc == 7))

        def tile_prog(i, samp, first, last, part):
            xs_t, xs_b = XS[i % 2]
            p_t, p_b = PS[i % 2]
            par = i % 2
            mL, b_mL = (mL_s, b_mL_s) if samp else (mL_p, b_mL_p)
            mU, b_mU = (mU_s, b_mU_s) if samp else (mU_p, b_mU_p)
            rm, b_rm = (rm_s, b_rm_s) if samp else (rm_p, b_rm_p)
            if part == 'front':
                rms_rows(xs_t, xs_b, xn, b_xn, xn, b_xn)
                transpose8(xn, b_xn, xnT, b_xnT)
                for grp in ([12], [0, 1, 2, 3], [4, 5, 6, 7], [8, 9, 10, 11]):
                    bk, bb = pbank()
                    for s, j in enumerate(grp):
                        inproj(bk, bb, s, j * 128)
                    n = len(grp)
                    for s, j in enumerate(grp):
                        ACT(tmpS[:, s, 1:129], bk[:, s * 128:(s + 1) * 128], AF.Copy, [bb, b_mu], [b_tmpS], scale=mu_t[:, j:j + 1])
                        if samp:
                            ACT(tmpS[:, s, 0:128:8], shT[:, j, :], AF.Copy, [b_shT, b_mu], [b_tmpS], scale=mu_t[:, j:j + 1])
                    if not samp:
                        CP('act', tmpS[:, 0:n, 0], carry[:, grp[0]:grp[0] + n], [b_carry], [b_tmpS])
                    for s, j in enumerate(grp):
                        STT(fsA[:, j, :], bk[:, s * 128:(s + 1) * 128], omu[:, j:j + 1], tmpS[:, s, 0:128], ALU.mult, ALU.add,
                            [bb, b_omu, b_tmpS], [b_fs])
                    if not samp:
                        CP('act', carry[:, grp[0]:grp[0] + n], tmpS[:, 0:n, 128], [b_tmpS], [b_carry])
                        if last:
                            CP('act', fl[:, grp[0]:grp[0] + n], bk[:, 0:n * 128].rearrange("p (a b) -> p a b", b=128)[:, :, 127], [bb], [b_fl])
                    else:
                        for s, j in enumerate(grp):
                            CP('act', fls[:, j, :], bk[:, s * 128 + 7:(s + 1) * 128:8], [bb], [b_fls])
                return
            if stop == 's2a':
                return
            if stop == 's2':
                return
            fl2 = lambda t: t[:].rearrange("p a b -> p (a b)")
            rF, kF, vF = fsA[:, 0:4, :], fsA[:, 4:8, :], fsA[:, 8:12, :]
            ACT(lorain[0:64, :], fsA[0:64, 12, :], AF.Tanh, [b_fs], [b_lorain])
            ACT(lorain[64:128, :], fsA[64:128, 12, :], AF.Copy, [b_fs], [b_lorain])
            bD, bbD = pbank()
            bA, bbA = pbank()
            for c in range(4):
                MM(bD[:, c * 128:(c + 1) * 128], wl_b[0:64, c * 128:(c + 1) * 128], lorain[0:64, :], [b_wl, b_lorain], [bbD])
            for c in range(4):
                MM(bA[:, c * 128:(c + 1) * 128], wl_b[64:128, c * 128:(c + 1) * 128], lorain[64:128, :], [b_wl, b_lorain], [bbA])
            for c in range(4):
                ACT(tdec[:, c, :], bD[:, c * 128:(c + 1) * 128], AF.Tanh, [bbD, b_hw0], [b_tdec], scale=0.5, bias=hw0[:, c:c + 1])
                ACT(ta[:, c, :], bA[:, c * 128:(c + 1) * 128], AF.Tanh, [bbA, b_ha0], [b_ta], scale=0.5, bias=ha0[:, c:c + 1])
            TS('dve', fl2(tdec), fl2(tdec), DEC_C, ALU.mult, [b_tdec], [b_tdec], s2=DEC_C, op1=ALU.add)
            S.op('dve', lambda e: e.tensor_tensor_scan(out=fl2(cum), data0=fl2(rm), data1=fl2(tdec), initial=0.0, op0=ALU.mult, op1=ALU.add),
                 reads=[b_rm, b_tdec], writes=[b_cum])
            TT('dve', fl2(dif), fl2(cum), fl2(tdec), ALU.subtract, [b_cum, b_tdec], [b_dif])
            ACT(fl2(Pin), fl2(cum), AF.Exp, [b_cum], [b_Pin])
            ACT(fl2(Piv), fl2(cum), AF.Exp, [b_cum], [b_Piv], scale=-1.0)
            ACT(fl2(Pex), fl2(dif), AF.Exp, [b_dif], [b_Pex])
            bk, bb = pbank()
            for s in range(4):
                inproj(bk, bb, s, C_ZR + s * 128)
            ACT(thz[:].rearrange("p a b -> p (a b)"), bk[:], AF.Tanh, [bb], [b_thz], scale=0.5)
            STT(gr[:].rearrange("p a b -> p (a b)"), thz[:].rearrange("p a b -> p (a b)"), 1.0, bk[:], ALU.add, ALU.mult, [b_thz, bb], [b_gr])
            bk, bb = pbank()
            for s in range(4):
                inproj(bk, bb, s, C_ZA + s * 128)
            ACT(thz[:].rearrange("p a b -> p (a b)"), bk[:], AF.Tanh, [bb], [b_thz], scale=0.5)
            STT(ga[:].rearrange("p a b -> p (a b)"), thz[:].rearrange("p a b -> p (a b)"), 1.0, bk[:], ALU.add, ALU.mult, [b_thz, bb], [b_ga])

            bq, bbq = pbank()
            for s in range(4):
                inproj(bq, bbq, s, C_Q + s * 128)
            ACT(sqb[:].rearrange("p a b -> p (a b)"), bq[:], AF.Square, [bbq], [b_sqb])
            bk, bb = pbank()
            MM(bk[:], bd[:], sqb[:].rearrange("p a b -> p (a b)"), [b_bd, b_sqb], [bb])
            RSQ(rn[:].rearrange("p a b -> p (a b)"), bk[:], 1.0 / 64, 0, [bb], [b_rn])
            STT(qT[:].rearrange("p a b -> p (a b)"), bq[:], qw8[:, 0:1], rn[:].rearrange("p a b -> p (a b)"), ALU.mult, ALU.mult,
                [bbq, b_qw8, b_rn], [b_qT])
            kT, b_kT = kTs[par]
            vtm, b_vtm = vtms[par]
            bkv, bbkv = pbank()
            inproj(bkv, bbkv, 0, C_KA)
            inproj(bkv, bbkv, 1, C_VA)
            ACT(sq1[:], bkv[:, 0:128], AF.Square, [bbkv], [b_sq1])
            bk, bb = pbank()
            MM(bk[:, 0:128], bd[:], sq1[:], [b_bd, b_sq1], [bb])
            RSQ(rn1[:], bk[:, 0:128], 1.0 / 64, 0, [bb], [b_rn1])
            STT(kTf[:], bkv[:, 0:128], kw_t[:, 0:1], rn1[:], ALU.mult, ALU.mult, [bbkv, b_kw, b_rn1], [b_kTf])
            CP('act', kT[:], kTf[:], [b_kTf], [b_kT])
            CP('act', vTf[:], bkv[:, 128:256], [bbkv], [b_vTf])
            bk, bb = pbank()
            TR(bk[:, 0:128], vTf[:], identf[:], [b_vTf, b_identf], [bb])
            CP('act', vtm_f[:], bk[:, 0:128], [bb], [b_vtmf])
            CP('dve', vtm[:], bk[:, 0:128], [bb], [b_vtm])
            if samp or last:
                bk, bb = pbank()
                TR(bk[:, 0:128], kTf[:], identf[:], [b_kTf, b_identf], [bb])
                CP('act', ktm_f[:], bk[:, 0:128], [bb], [b_ktmf])
            for c in range(4):
                TS('dve', kk[:, c, :], fsA[:, 4 + c, :], kk_t[:, c:c + 1], ALU.mult, [b_fs, b_kkc], [b_kk])
                TS('dve', km1[:, c, :], ta[:, c, :], c2[:, c:c + 1], ALU.mult, [b_ta, b_c2, b_c1], [b_km1], s2=c1[:, c:c + 1], op1=ALU.add)
            ACT(fl2(sqb), fl2(kk), AF.Square, [b_kk], [b_sqb])
            bk, bb = pbank()
            MM(bk[:], bd[:], fl2(sqb), [b_bd, b_sqb], [bb])
            RSQ(fl2(rn), bk[:], 1.0, 1, [bb], [b_rn])
            TT('dve', fl2(kkn), fl2(kk), fl2(rn), ALU.mult, [b_kk, b_rn], [b_kkn])
            TT('dve', kmod[:], kF, km1[:], ALU.mult, [b_fs, b_km1], [b_kmod])
            TT('dve', Rt[:], rF, Pin[:], ALU.mult, [b_fs, b_Pin], [b_Rt])
            STT(fl2(At), fl2(kkn), -1.0, fl2(Pex), ALU.mult, ALU.mult, [b_kkn, b_Pex], [b_At])
            TS('dve', fl2(alr), fl2(ta), 0.5, ALU.mult, [b_ta], [b_alr], s2=0.5, op1=ALU.add)
            TT('dve', fl2(b2t), fl2(kkn), fl2(alr), ALU.mult, [b_kkn, b_alr], [b_b2t])
            TT('dve', fl2(Bt), fl2(b2t), fl2(Piv), ALU.mult, [b_b2t, b_Piv], [b_Bt])
            TT('dve', fl2(Kt), fl2(kmod), fl2(Piv), ALU.mult, [b_kmod, b_Piv], [b_Kt])
            for c in range(4):
                STT(rkb[:, c, :], fsA[:, c, :], rk_t[:, c:c + 1], kmod[:, c, :], ALU.mult, ALU.mult, [b_fs, b_rk, b_kmod], [b_rkb])
            bk, bb = pbank()
            MM(bk[:], bd[:], fl2(rkb), [b_bd, b_rkb], [bb])
            TT('dve', bonus[:], bk[:].rearrange("p (a b) -> p a b", b=128), vF, ALU.mult, [bb, b_fs], [b_bonus])
            CP('act', vbf[:], vF, [b_fs], [b_vbf])
            for (srcs, dsts) in (((At, b_At, Atm, b_Atm), (Bt, b_Bt, Btm, b_Btm)), ((Kt, b_Kt, Ktm, b_Ktm), (vbf, b_vbf, Vtm, b_Vtm))):
                bk, bb = pbank()
                bkb = bk[:].bitcast(BF16)
                for n_, (src, bsrc, dst, bdst) in enumerate((srcs, dsts)):
                    for c in range(4):
                        TR(bkb[:, n_ * 512 + c * 128:n_ * 512 + (c + 1) * 128], src[:, c, :], ident[:], [bsrc, b_ident], [bb])
                for n_, (src, bsrc, dst, bdst) in enumerate((srcs, dsts)):
                    CP('act' if n_ == 0 else 'dve', fl2(dst), bkb[:, n_ * 512:(n_ + 1) * 512], [bb], [bdst])

            if stop == 's3':
                return
            nlev = 3 if samp else 5
            ngrp = 4 if samp else 1
            cm, b_cm = (cm_s, b_cm_s) if samp else (cm_p, b_cm_p)
            csz = 8 if samp else 32
            bY, bbY = banks[7]
            MM(bY[:], zeros_b[:], Win[:, 0, 0:512], [b_zeros, b_Win], [bbY], start=True, stop=True)
            G2 = (0, 1)

            def a_kind(g, L, bL, Rr, bR, dst, bdst, mask, bmask, lo=0, hi=128):
                for e in range(2):
                    bk, bb = pbank()
                    for cl in range(2):
                        c = 2 * g + cl
                        MM(bk[:, cl * 128:(cl + 1) * 128], L[64 * e:64 * e + 64, c, :], Rr[64 * e:64 * e + 64, c, :], [bL, bR], [bb])
                    TT('dve', dst[:, 4 * g + e:4 * g + 4:2, lo:hi], bk[:, 0:256].rearrange("p (a b) -> p a b", b=128),
                       mask[:].unsqueeze(1).broadcast_to([128, 2, 128]), ALU.mult, [bb, bmask], [bdst.parts[g]])
            for g in G2:
                a_kind(g, At, b_At, Bt, b_Bt, X, b_X, mL, b_mL)
                a_kind(g, Bt, b_Bt, Rt, b_Rt, Rh, b_Rh, mU, b_mU, 64, 192)
                CP('act', Rh[:, 4 * g:4 * g + 4, 0:64], Btm[:, 2 * g:2 * g + 2, :].rearrange("p c (e k) -> p (c e) k", k=64), [b_Btm.parts[g]], [b_Rh.parts[g]])
                S.op('dve', lambda e, g=g: e.transpose(out=XTf[:, 512 * g:512 * g + 512], in_=Xf[:, 512 * g:512 * g + 512]),
                     reads=[b_X.parts[g]], writes=[b_XT.parts[g]])
            for lev in range(nlev):
                if not samp and lev < 4:
                    c = lev
                    TT('dve', Vm[:, c], Vtm[:, c, :].unsqueeze(1).broadcast_to([128, 4, 128]), cm[:, 0:4].unsqueeze(2).broadcast_to([128, 4, 128]),
                       ALU.mult, [b_Vtm, b_cm], [b_Vm.parts[c // 2]])
                for g in G2:
                    if samp:
                        if lev == 1:
                            a_kind(g, At, b_At, Kt, b_Kt, Aak, b_Aak, mL, b_mL)
                            a_kind(g, Kt, b_Kt, Rt, b_Rt, ArkT, b_ArkT, mU, b_mU)
                    else:
                        if lev == 1 + g:
                            a_kind(g, At, b_At, Kt, b_Kt, Aak, b_Aak, mL, b_mL)
                        if lev == 2 + g:
                            a_kind(g, Kt, b_Kt, Rt, b_Rt, ArkT, b_ArkT, mU, b_mU)
                    bX, bXT, bR = b_X.parts[g], b_XT.parts[g], b_Rh.parts[g]
                    appb = []
                    for j in range(2):
                        bk, bb = pbank()
                        appb.append((bk, bb))
                        for hh in range(2):
                            h = 4 * g + 2 * j + hh
                            MM(bk[:, hh * 192:(hh + 1) * 192], X[:, h, :], Rh[:, h, :], [bX, bR], [bb], start=True, stop=(j == 1),
                               sig=(j == 1 and hh == 1))
                            if j == 0:
                                MM(bk[:, hh * 192:(hh + 1) * 192], ident[:], Rh[:, h, :], [b_ident, bR], [bb], start=False, stop=True, sig=(hh == 1))
                    if lev < nlev - 1:
                        bs, bbs = pbank()
                        for hh in range(4):
                            h = 4 * g + hh
                            MM(bs[:, hh * 128:(hh + 1) * 128], XT[:, h, :], X[:, h, :], [bXT, bX], [bbs], sig=(hh == 3))
                    for j in range(2):
                        bk, bb = appb[j]
                        h0 = 4 * g + 2 * j
                        if j == 0:
                            CP('act', Rh[:, h0:h0 + 2, :].rearrange("p a b -> p (a b)"), bk[:, 0:384], [bb], [bR])
                        else:
                            TT('dve', Rh[:, h0:h0 + 2, :], bk[:, 0:384].rearrange("p (a b) -> p a b", b=192), Rh[:, h0:h0 + 2, :], ALU.add,
                               [bb, bR], [bR])
                    if lev < nlev - 1:
                        CP('act', Xf[:, 512 * g:512 * g + 512], bs[:], [bbs], [bX])
                        S.op('dve', lambda e, g=g: e.transpose(out=XTf[:, 512 * g:512 * g + 512], in_=Xf[:, 512 * g:512 * g + 512]),
                             reads=[bX], writes=[bXT])
            for g in G2:
                bR = b_Rh.parts[g]
                for jl in range(2):
                    j = 2 * g + jl
                    bk, bb = pbank()
                    for hh in range(2):
                        h = 2 * j + hh
                        MM(bk[:, hh * 192:(hh + 1) * 192], Aak[:, h, :], Rh[:, h, :], [b_Aak.parts[g], bR], [bb])
                    v = bk[:, 0:384].rearrange("p (a b) -> p a b", b=192)
                    TT('dve', Khat[:, j, :].rearrange("p (e k) -> p e k", k=64), v[:, :, 0:64], Ktm[:, j, :].rearrange("p (e k) -> p e k", k=64), ALU.add,
                       [bb, b_Ktm.parts[g]], [b_Ktm.parts[g]])
                    TT('dve', AhT[:, 2 * j:2 * j + 2, :], v[:, :, 64:192], ArkT[:, 2 * j:2 * j + 2, :], ALU.add, [bb, b_ArkT.parts[g]], [b_ArkT.parts[g]])
                bk, bb = pbank()
                for hh in range(4):
                    h = 4 * g + hh
                    c, e = h // 2, h % 2
                    cl = c - 2 * g
                    MM(bk[64 * e:64 * e + 64, cl * 128:(cl + 1) * 128], Atm[:, c, 64 * e:64 * e + 64], Rh[:, h, 64:192], [b_Atm, bR], [bb],
                       tp=(0, 64 * e))
                TT('dve', RhT[:, 2 * g:2 * g + 2, :], bk[:, 0:256].rearrange("p (a b) -> p a b", b=128), Rt[:, 2 * g:2 * g + 2, :], ALU.add,
                   [bb, b_Rt.parts[g]], [b_Rt.parts[g]])
                CP('act', Zc[:, 2 * g:2 * g + 2, :].rearrange("p c (e k) -> p (c e) k", k=64), Rh[:, 4 * g:4 * g + 4, 0:64], [bR], [b_Btm.parts[g]])
            att = {}
            def attnA():
                blks = []
                if samp:
                    blks.append((kT, b_kT, vtm, b_vtm, mU_s, b_mU_s))
                else:
                    if not first:
                        kTp, b_kTp = kTs[1 - par]
                        vtp, b_vtp = vtms[1 - par]
                        blks.append((kTp, b_kTp, vtp, b_vtp, mAL, b_mAL))
                    blks.append((kT, b_kT, vtm, b_vtm, mAU, b_mAU))
                elist = []
                ei = 0
                for (kt_, bkt_, vt_, bvt_, mk_, bmk_) in blks:
                    for kv in range(2):
                        bk, bb = pbank()
                        MM(bk[:], kt_[64 * kv:64 * kv + 64, :], qT[64 * kv:64 * kv + 64, :, :], [bkt_, b_qT], [bb])
                        E_t, E_b = Es[ei]
                        ei += 1
                        ACT(E_t[:], bk[:], AF.Exp, [bb], [E_b])
                        TT('dve', E_t[:].rearrange("p (a b) -> p a b", b=128), E_t[:].rearrange("p (a b) -> p a b", b=128),
                           mk_[:].unsqueeze(1).broadcast_to([128, 4, 128]), ALU.mult, [E_b, bmk_], [E_b])
                        elist.append((kv, E_t, E_b, vt_, bvt_))
                att['elist'] = elist
            def attnB():
                elist = att['elist']
                bN, bbN = pbank()
                bDn, bbDn = pbank()
                if samp:
                    MM(bN[:], zeros_b[:], Win[:, 0, 0:512], [b_zeros, b_Win], [bbN], start=True, stop=True)
                for kv in range(2):
                    mine = [x for x in elist if x[0] == kv]
                    for n_, (_, E_t, E_b, vt_, bvt_) in enumerate(mine):
                        MM(bN[64 * kv:64 * kv + 64, :], vt_[:, 64 * kv:64 * kv + 64], E_t[:], [bvt_, E_b], [bbN], start=(n_ == 0 and not samp), stop=True, tp=(0, 64 * kv))
                        MM(bDn[64 * kv:64 * kv + 64, :], ones_b[:, 0:64], E_t[:], [b_ones, E_b], [bbDn], start=(n_ == 0), stop=True, tp=(0, 64 * kv))
                if samp:
                    bDc, bbDc = pbank()
                    for kv in range(2):
                        bk, bb = pbank()
                        for b in range(16):
                            MM(bk[:, b * 32:(b + 1) * 32].rearrange("p (g j) -> p g j", j=8), ckT[64 * kv:64 * kv + 64, b, :],
                               qT[64 * kv:64 * kv + 64, :, b * 8:(b + 1) * 8], [b_ckT, b_qT], [bb])
                        Ec_t, Ec_b = Ec[kv]
                        ACT(Ec_t[:], bk[:], AF.Exp, [bb], [Ec_b])
                        TT('dve', Ec_t[:].rearrange("p (a j) -> p a j", j=8), Ec_t[:].rearrange("p (a j) -> p a j", j=8),
                           mAL[:, 0:8].unsqueeze(1).broadcast_to([128, 64, 8]), ALU.mult, [Ec_b, b_mAL], [Ec_b])
                        for b in range(16):
                            MM(bN[64 * kv:64 * kv + 64, :].rearrange("p (g t) -> p g t", t=128)[:, :, b * 8:(b + 1) * 8],
                               cvb[:, b, 64 * kv:64 * kv + 64], Ec_t[:, b * 32:(b + 1) * 32].rearrange("p (g j) -> p g j", j=8),
                               [b_cvb, Ec_b], [bbN], start=False, stop=True, tp=(0, 64 * kv))
                        MM(bDc[64 * kv:64 * kv + 64, :], ones_b[:, 0:64], Ec_t[:], [b_ones, Ec_b], [bbDc], tp=(0, 64 * kv))
                    CP('act', dcs[:].rearrange("p g (b j) -> p g b j", j=8), bDc[:].rearrange("p (b g j) -> p g b j", g=4, j=8), [bbDc], [b_dcs])
                att['banks'] = (bN, bbN, bDn, bbDn) + ((bDc, bbDc) if samp else ())
            def attnC():
                bN, bbN, bDn, bbDn = att['banks'][0:4]
                TT('dve', dent[:], bDn[:].rearrange("p (a b) -> p a b", b=128), esk[:].unsqueeze(2).broadcast_to([128, 4, 128]), ALU.add,
                   [bbDn, b_esk], [b_dent])
                if samp:
                    TT('dve', fl2(dent), fl2(dent), fl2(dcs), ALU.add, [b_dent, b_dcs], [b_dent])
                ACT(fl2(dent), fl2(dent), AF.Ln, [b_dent], [b_dent])
                ACT(fl2(dent), fl2(dent), AF.Exp, [b_dent], [b_dent], scale=-1.0)
                TT('dve', fl2(oa), bN[:], fl2(dent), ALU.mult, [bbN, b_dent], [b_oa])
                if i == 0:
                    dbg_dump('oa', oa[:], [b_oa], [128, 4, 128])
                STT(oT[:, 4:8, :].rearrange("p a b -> p (a b)"), fl2(oa), 0.5, fl2(ga), ALU.mult, ALU.mult, [b_oa, b_ga], [b_oT])
            for g_ in range(ngrp):
                cmg = cm[:, g_ * 4:(g_ + 1) * 4]
                for g in G2:
                    for cl in range(2):
                        c = 2 * g + cl
                        zt, zb = Zm(c)
                        TT('dve', zt, Zc[:, c, :].unsqueeze(1).broadcast_to([128, 4, 128]), cmg.unsqueeze(2).broadcast_to([128, 4, 128]),
                           ALU.mult, [b_Btm.parts[g], b_cm], [zb])
                        if samp:
                            TT('dve', Vm[:, c], Vtm[:, c, :].unsqueeze(1).broadcast_to([128, 4, 128]), cmg.unsqueeze(2).broadcast_to([128, 4, 128]),
                               ALU.mult, [b_Vtm, b_cm], [b_Vm.parts[g]])
                    for cl in range(2):
                        c = 2 * g + cl
                        zt, zb = Zm(c)
                        bk, bb = pbank()
                        MM(bk[:], Atm[:, c, :], zt.rearrange("p a b -> p (a b)"), [b_Atm, zb], [bb])
                        TT('dve', GTm[:, c], bk[:].rearrange("p (a b) -> p a b", b=128), bd[:].unsqueeze(1).broadcast_to([128, 4, 128]), ALU.mult,
                           [bb, b_bd], [b_GTm.parts[g]])
                for cc in range(4):
                    ch_id = g_ * 4 + cc
                    t0 = ch_id * csz
                    te = t0 + csz - 1
                    if samp:
                        sl = ch_id % 2
                        sbd_t, sbd_b = Sbd[sl]
                        bk0, bb0 = pbank()
                        for c in range(4):
                            TR(bk0[:, c * 128:(c + 1) * 128], sbd_t[:, c, :], identf[:], [sbd_b, b_identf], [bb0])
                        H0b_s, bH0b_s = ((Hbf, b_Hbf), (sqb, Same(b_sqb)))[sl]
                        HP_s, bHP_s = ((tB, b_tB), (tA, Same(b_tA)))[sl]
                        Ho_s, bHo_s = ((kmod, b_kmod), (tC, Same(b_tC)))[sl]
                        CP('act', fl2(H0b_s), bk0[:], [bb0], [bH0b_s])
                        for g in G2:
                            TT('dve', HP_s[:, 2 * g:2 * g + 2, :], bk0[:, 256 * g:256 * g + 256].rearrange("p (a b) -> p a b", b=128),
                               Pin[:, 2 * g:2 * g + 2, te].unsqueeze(2).broadcast_to([128, 2, 128]), ALU.mult, [bb0, b_Pin], [bHP_s.parts[g]])
                        if ch_id + 2 < 16:
                            load_state(ch_id + 2)
                        Hs_b, bHb = H0b_s, bH0b_s
                        Hn, b_Hn = Ho_s, bHo_s
                        HPu, bHPu = HP_s, bHP_s
                    else:
                        for g in G2:
                            TT('dve', HP[:, 2 * g:2 * g + 2, :], H[:, 2 * g:2 * g + 2, :],
                               Pin[:, 2 * g:2 * g + 2, te].unsqueeze(2).broadcast_to([128, 2, 128]), ALU.mult, [b_H.parts[g], b_Pin], [b_HP.parts[g]])
                        Hs_b, bHb = Hbf, b_Hbf
                        Hn, b_Hn = H, b_H
                        HPu, bHPu = HP, b_HP
                    for g in G2:
                        bAcc, bbAcc = pbank()
                        for cl in range(2):
                            c = 2 * g + cl
                            MM(bY[:, c * 128 + t0:c * 128 + t0 + csz], Hs_b[:, c, :], RhT[:, c, t0:t0 + csz], [bHb.parts[g], b_Rt.parts[g]], [bbY],
                               start=False, stop=True, sig=False)
                            MM(bAcc[:, cl * 128:(cl + 1) * 128], GTm[:, c, cc, :], Hs_b[:, c, :], [b_GTm.parts[g], bHb.parts[g]], [bbAcc],
                               start=True, stop=False, sig=False)
                            for e in range(2):
                                MM(bAcc[64 * e:64 * e + 64, cl * 128 + 64 * e:cl * 128 + 64 * e + 64], Khat[:, c, 64 * e:64 * e + 64],
                                   Vm[:, c, cc, 64 * e:64 * e + 64], [b_Ktm.parts[g], b_Vm.parts[g]], [bbAcc], start=False, stop=True, tp=(0, 64 * e),
                                   sig=(cl == 1 and e == 1))
                        if not samp:
                            for cl in range(2):
                                c = 2 * g + cl
                                STT(Hbf[:, c, :], bAcc[:, cl * 128:(cl + 1) * 128], Pin[:, c, te:te + 1], HPu[:, c, :], ALU.mult, ALU.add,
                                    [bbAcc, b_Pin, bHPu.parts[g]], [b_Hbf.parts[g]])
                        for cl in range(2):
                            c = 2 * g + cl
                            STT(Hn[:, c, :], bAcc[:, cl * 128:(cl + 1) * 128], Pin[:, c, te:te + 1], HPu[:, c, :], ALU.mult, ALU.add,
                                [bbAcc, b_Pin, bHPu.parts[g]], [b_Hn.parts[g]])
                    if not samp and cc < 3:
                        (attnA, attnB, attnC)[cc]()
                    if samp:
                        bk, bb = pbank()
                        for c in range(4):
                            TR(bk[:, c * 128:(c + 1) * 128], Hn[:, c, :], identf[:], [b_Hn, b_identf], [bb])
                        so_t, so_b = Sout[sl]
                        CP('act', fl2(so_t), bk[:], [bb], [so_b])
                        for e in range(2):
                            DMA(s_s[ch_id].rearrange("(c e) v k -> e v c k", e=2)[e], so_t[64 * e:64 * e + 64, :, 64 * e:64 * e + 64],
                                [so_b], [], Sout_st[sl])
            if stop == 'chain':
                return
            for h in range(8):
                c, e = h // 2, h % 2
                MM(bY[64 * e:64 * e + 64, c * 128:(c + 1) * 128], Vtm[:, c, 64 * e:64 * e + 64], AhT[:, h, :], [b_Vtm, b_AhT], [bbY],
                   start=False, stop=True, tp=(0, 64 * e))
            if stop == 'p5':
                return
            CP('act', fl2(ysb), bY[:], [bbY], [b_ysb])
            CP('act', fl2(ybf), fl2(ysb), [b_ysb], [b_ybf])
            if i == 0:
                dbg_dump('ysb', ysb[:], [b_ysb], [128, 4, 128])
            bk, bb = pbank()
            MM(bk[:], bd[:], fl2(ybf), [b_bd, b_ybf], [bb])
            STT(fl2(yc), bk[:], -1.0 / 64, fl2(ysb), ALU.mult, ALU.add, [bb, b_ysb], [b_yc])
            ACT(fl2(sqb), fl2(yc), AF.Square, [b_yc], [b_sqb])
            bk, bb = pbank()
            MM(bk[:], bd[:], fl2(sqb), [b_bd, b_sqb], [bb])
            RSQ(fl2(rn), bk[:], 1.0 / 64, 2, [bb], [b_rn])
            TT('dve', fl2(yc), fl2(yc), fl2(rn), ALU.mult, [b_yc, b_rn], [b_yc])
            for c in range(4):
                TS('dve', yo[:, c, :], yc[:, c, :], lw_t[:, c:c + 1], ALU.mult, [b_yc, b_lnw, b_lnb], [b_yo], s2=lb_t[:, c:c + 1], op1=ALU.add)
            TT('dve', fl2(yo), fl2(yo), fl2(bonus), ALU.add, [b_yo, b_bonus], [b_yo])
            if i == 0:
                dbg_dump('yo', yo[:], [b_yo], [128, 4, 128])
            STT(oT[:, 0:4, :].rearrange("p a b -> p (a b)"), fl2(yo), 0.5, fl2(gr), ALU.mult, ALU.mult, [b_yo, b_gr], [b_oT])

            if stop == 's5':
                return
            if samp:
                attnA(); attnB(); attnC()
            return

        def tile_tail(i, samp):
            xs_t, xs_b = XS[i % 2]
            p_t, p_b = PS[i % 2]
            for n_ in range(2):
                bk, bb = pbank()
                for kc in range(8):
                    MM(bk[:], oT[:, kc, :], Wout[:, kc, n_ * 512:(n_ + 1) * 512], [b_oT, b_Wout], [bb], start=(kc == 0), stop=(kc == 7), sig=(kc == 7))
                TT('dve', xs_t[:, n_ * 512:(n_ + 1) * 512], bk[:], xs_t[:, n_ * 512:(n_ + 1) * 512], ALU.add, [bb, xs_b], [xs_b])
            if i == 0:
                dbg_dump('h', xs_t[:], [xs_b], [128, 1024])
            rms_rows(xs_t, xs_b, hn, b_hn, hn, b_hn, cols=1)
            transpose8(hn, b_hn, hnT, b_hnT)
            CP('act', pbf[:], p_t[:], [p_b], [b_pbf])
            bk, bb = pbank()
            bkb = bk[:].bitcast(BF16)
            for kc in range(2):
                TR(bkb[:, kc * 128:(kc + 1) * 128], pbf[:, kc * 128:(kc + 1) * 128], ident[:], [b_pbf, b_ident], [bb])
            CP('dve', pT[:].rearrange("p a b -> p (a b)"), bkb[:, 0:256], [bb], [b_pT])
            for n_ in range(2):
                bk, bb = pbank()
                for kc in range(8):
                    MM(bk[:], hnT[:, kc, :], Wg[:, kc, n_ * 512:(n_ + 1) * 512], [b_hnT, b_Wg], [bb], start=(kc == 0), stop=(kc == 7), sig=(kc == 7))
                ACT(tg[:, n_ * 512:(n_ + 1) * 512], bk[:], AF.Tanh, [bb], [b_tg], scale=0.5)
                bk2, bb2 = pbank()
                for kc in range(2):
                    MM(bk2[:], pT[:, kc, :], Wp[:, kc, n_ * 512:(n_ + 1) * 512], [b_pT, b_Wp], [bb2], start=(kc == 0), stop=(kc == 1))
                STT(tg[:, n_ * 512:(n_ + 1) * 512], tg[:, n_ * 512:(n_ + 1) * 512], 1.0, bk2[:], ALU.add, ALU.mult, [b_tg, bb2], [b_tg])
                STT(xs_t[:, n_ * 512:(n_ + 1) * 512], tg[:, n_ * 512:(n_ + 1) * 512], 0.5, xs_t[:, n_ * 512:(n_ + 1) * 512], ALU.mult, ALU.add,
                    [b_tg, xs_b], [xs_b])
            if samp:
                DMA(y_s, xs_t[:], [xs_b], [], XS_st[i % 2])
            else:
                DMA(y_p[i * 128:(i + 1) * 128, :], xs_t[:], [xs_b], [], XS_st[i % 2])

        if do_sample:
            cst = SoutAll[:].rearrange("p s a b -> p (s a) b")
            for n_ in range(8):
                isk = n_ < 4
                src_c = (c_k if isk else c_v)
                b0 = (n_ % 4) * 4
                half = cst[:, (n_ % 2) * 4:(n_ % 2) * 4 + 4, :]
                DMA(half, src_c[b0:b0 + 4].rearrange("b i d -> i b d"), [], [b_SoutAll], ld_c)
                if isk:
                    cbs = xn[:, (n_ % 2) * 512:(n_ % 2) * 512 + 512]
                    CP('act', cbs.rearrange("p (a b) -> p a b", b=128), half, [b_SoutAll], [b_xn])
                    bk, bb = pbank()
                    bkb = bk[:].bitcast(BF16)
                    for b in range(4):
                        TR(bkb[:, b * 128:(b + 1) * 128], cbs[:, b * 128:(b + 1) * 128], ident[:], [b_xn, b_ident], [bb])
                    CP('act', ckT[:, b0:b0 + 4, :].rearrange("p a b -> p (a b)"), bkb[:, 0:512], [bb], [b_ckT])
                else:
                    CP('act', cvb[:, b0:b0 + 4, :], half, [b_SoutAll], [b_cvb])
        load_tile(0, False)
        if do_sample:
            st_cp = S.dma_chan("st_cp")
            DMA(ck_s[:, 0:120, :], c_k[:, 8:128, :], [], [], st_cp)
            DMA(cv_s[:, 0:120, :], c_v[:, 8:128, :], [], [], st_cp)
        tile_prog(0, False, True, ntile == 1, 'front')
        for i in range(0 if stop == 'w' else ntile):
            if i + 1 < ntile:
                load_tile(i + 1, False)
            elif do_sample:
                load_tile(ntile, True)
            tile_prog(i, False, i == 0, i == ntile - 1, 'mid')
            if i + 1 < ntile:
                tile_prog(i + 1, False, False, i + 1 == ntile - 1, 'front')
            elif do_sample:
                tile_prog(ntile, True, False, False, 'front')
            tile_tail(i, False)
        bk, bb = pbank()
        for c in range(4):
            TR(bk[:, c * 128:(c + 1) * 128], H[:, c, :], identf[:], [b_H, b_identf], [bb])
        CP('act', Hout[:].rearrange("p a b -> p (a b)"), bk[:], [bb], [b_Hout])
        for e in range(2):
            DMA(s_p.rearrange("(c e) v k -> e v c k", e=2)[e], Hout[64 * e:64 * e + 64, :, 64 * e:64 * e + 64], [b_Hout], [], st_misc)
        bk, bb = pbank()
        TR(bk[0:13, 0:128], fl[:, 0:13], identf[:], [b_fl, b_identf], [bb])
        CP('act', colsT[0:13, :], bk[0:13, 0:128], [bb], [b_colsT])
        DMA(sh_p.rearrange("(c p) -> c p", p=128), colsT[0:13, :], [b_colsT], [], st_misc)
        DMA(ck_p, ktm_f[:], [b_ktmf], [], st_misc)
        DMA(cv_p, vtm_f[:], [b_vtmf], [], st_misc)
        for b_ in _flat([b_Hout, b_colsT, b_ktmf, b_vtmf]):
            b_.r[st_misc.name] = (st_misc, st_misc.count)

        if do_sample:
            for sl in range(2):
                S.op('dve', lambda e, sl=sl: e.memset(Sbd[sl][0][:], 0.0), writes=[Sbd[sl][1]])

            def load_state(s_):
                sl = s_ % 2
                t_, b_ = Sbd[sl]
                for e in range(2):
                    DMA(t_[64 * e:64 * e + 64, :, 64 * e:64 * e + 64], st_r[s_].rearrange("(c e) v k -> e v c k", e=2)[e], [], [b_], Sbd_ld[sl])
            load_state(0)
            load_state(1)
            if stop != 'w':
                tile_prog(ntile, True, False, False, 'mid')
                tile_tail(ntile, True)
                for q4 in range(4):
                    bk, bb = pbank()
                    for j in range(q4 * 4, min(13, q4 * 4 + 4)):
                        TR(bk[0:16, (j % 4) * 128:(j % 4 + 1) * 128], fls[:, j, :], identf[:], [b_fls, b_identf], [bb])
                    nj = min(13, q4 * 4 + 4) - q4 * 4
                    CP('act' if q4 % 2 == 0 else 'dve', fsA[0:16, q4 * 4:q4 * 4 + nj, :].rearrange("p a b -> p (a b)"), bk[0:16, 0:nj * 128], [bb], [b_fs])
                DMA(sh_s.rearrange("b (c p) -> b c p", p=128), fsA[0:16, :, :], [b_fs], [], st_misc)
            for b in range(16):
                DMA(ck_s[b, 120:128, :], ktm_f[b * 8:(b + 1) * 8, :], [b_ktmf], [], st_misc)
                DMA(cv_s[b, 120:128, :], vtm_f[b * 8:(b + 1) * 8, :], [b_vtmf], [], st_misc)
        for ch in S.dchans:
            if ch.name.startswith("st_") and ch.count:
                nc.sync.wait_ge(ch.sem, ch.count)
    return nc, dram_out


_CACHE = {}


def kernel(**inp):
    f = lambda a: np.ascontiguousarray(np.asarray(a, dtype=np.float32))
    if "nc" not in _CACHE:
        _CACHE["nc"] = build_nc()
    nc, _ = _CACHE["nc"]
    shared = {
        "g_norm": f(inp["g_norm"][0]), "w_in": f(inp["w_in"][0]), "mu_shift": f(inp["mu_shift"][0]),
        "w0": f(inp["w0"][0]), "w_dec2": f(inp["w_dec2"][0]), "a0": f(inp["a0"][0]), "w_a2": f(inp["w_a2"][0]),
        "k_k": f(inp["k_k"][0]), "k_a": f(inp["k_a"][0]), "r_k": f(inp["r_k"][0]).reshape(512),
        "lnx_w": f(inp["lnx_w"][0]), "lnx_b": f(inp["lnx_b"][0]), "q_norm_w": f(inp["q_norm_w"][0]),
        "k_norm_w": f(inp["k_norm_w"][0]), "sinks": f(inp["sinks"][0]), "w_out": f(inp["w_out"][0]),
        "g_ple": f(inp["g_ple"][0]), "w_gate": f(inp["w_ple_gate"][0]), "w_ple": f(inp["w_ple_proj"][0]),
    }
    in_maps = []
    for c in range(NCORES):
        sl = slice(16 * c, 16 * c + 16)
        m = dict(shared)
        m["x_p"] = f(inp["x_prompt"][c]); m["x_s"] = f(inp["x_sample"][sl]).reshape(128, D)
        m["st_r"] = f(inp["state_rwkv"][0, sl]); m["st_sh"] = f(inp["state_shift"][0, sl, 0])
        m["c_k"] = f(inp["cache_k"][0, sl]).reshape(16, 128, 128); m["c_v"] = f(inp["cache_v"][0, sl]).reshape(16, 128, 128)
        m["p_p"] = f(inp["p_prompt"][0, c]); m["p_s"] = f(inp["p_sample"][0, sl]).reshape(128, 256)
        in_maps.append(m)
    res = run_bass_kernel_spmd(nc, in_maps, core_ids=list(range(NCORES)))
    R = res.results
    g = lambda k: [np.asarray(R[c][k], dtype=np.float32) for c in range(NCORES)]
    y_p = np.stack(g("y_p"))
    y_s = np.concatenate(g("y_s")).reshape(128, 8, D)
    s_p = np.stack(g("s_p"))[None]
    s_s = np.concatenate(g("s_s"))[None]
    sh_p = np.stack(g("sh_p")).reshape(1, 8, 1, DSH)
    sh_s = np.concatenate(g("sh_s")).reshape(1, 128, 1, DSH)
    ck_p = np.stack(g("ck_p")).reshape(1, 8, 128, 2, 64)
    ck_s = np.concatenate(g("ck_s")).reshape(1, 128, 128, 2, 64)
    cv_p = np.stack(g("cv_p")).reshape(1, 8, 128, 2, 64)
    cv_s = np.concatenate(g("cv_s")).reshape(1, 128, 128, 2, 64)
    return (y_p, y_s, s_p, s_s, sh_p, sh_s, ck_p, ck_s, cv_p, cv_s)
```
